# Optimizing a Trainium2 kernel written in Bass

```python
import jax, jax.numpy as jnp
from jax import lax
import numpy as np

D_MODEL = 1024
BATCH = 8
SEQ = 4096
DEPTH = 2

CHUNK = 64
Q_BLOCK = 128
D_MIX = D_MODEL
N_HEADS = 8
QK_NOPE = 64
QK_ROPE = 32
QK_DIM = QK_NOPE + QK_ROPE
V_DIM = 64
Q_LORA = 384
KV_LORA = 256
ATTN_W = N_HEADS * V_DIM
CONV_W = D_MIX - ATTN_W
CONV_K = 31
D_FF = 2816
ROPE_THETA = 10000.0
EPS = 1e-6
IN_COLS = Q_LORA + KV_LORA + QK_ROPE + 2 * CONV_W

kernel_name = "hymba_mla_conformer_conv_macaron"


def rms_norm(x, g):
    xf = x.astype(jnp.float32)
    y = xf * lax.rsqrt(jnp.mean(xf * xf, axis=-1, keepdims=True) + EPS)
    return (y * g.astype(jnp.float32)).astype(x.dtype)


def layer_norm(x, g, b):
    xf = x.astype(jnp.float32)
    mu = jnp.mean(xf, axis=-1, keepdims=True)
    xc = xf - mu
    y = xc * lax.rsqrt(jnp.mean(xc * xc, axis=-1, keepdims=True) + EPS)
    return (y * g.astype(jnp.float32) + b.astype(jnp.float32)).astype(x.dtype)


def swiglu(x, w_gate, w_up, w_down):
    return (jax.nn.silu(x @ w_gate) * (x @ w_up)) @ w_down


def rope_tables(seq_len):
    pos = jnp.arange(seq_len, dtype=jnp.float32)
    inv_freq = 1.0 / (ROPE_THETA ** (jnp.arange(0, QK_ROPE, 2, dtype=jnp.float32) / QK_ROPE))
    ang = pos[:, None] * inv_freq[None, :]
    return jnp.cos(ang), jnp.sin(ang)


def apply_rope(x, cos, sin):
    xf = x.astype(jnp.float32)
    half = QK_ROPE // 2
    x1, x2 = xf[..., :half], xf[..., half:]
    c, s = cos[None, :, None, :], sin[None, :, None, :]
    out = jnp.concatenate([x1 * c - x2 * s, x1 * s + x2 * c], axis=-1)
    return out.astype(x.dtype)


def chunk_causal_attention(q, k, v):
    b, h, s, _ = q.shape
    scale = QK_DIM ** -0.5
    outs = []
    for i in range(s // Q_BLOCK):
        q0 = i * Q_BLOCK
        k_end = q0 + Q_BLOCK
        qs = q[:, :, q0:k_end]
        ks = k[:, :, :k_end]
        vs = v[:, :, :k_end]
        scores = jnp.einsum('bhqd,bhkd->bhqk', qs, ks).astype(jnp.float32) * scale
        q_chunk = (q0 + jnp.arange(Q_BLOCK)) // CHUNK
        k_chunk = jnp.arange(k_end) // CHUNK
        allowed = k_chunk[None, :] <= q_chunk[:, None]
        scores = jnp.where(allowed[None, None], scores, -1e30)
        probs = jax.nn.softmax(scores, axis=-1).astype(vs.dtype)
        outs.append(jnp.einsum('bhqk,bhkd->bqhd', probs, vs))
    out = jnp.concatenate(outs, axis=1)
    return out.reshape(b, s, h * V_DIM)


def mla_group(c_q_raw, c_kv_raw, k_pe_raw, q_lat_g, w_uq, kv_lat_g, w_ukv, q_g, k_g, cos, sin):
    b, s, _ = c_q_raw.shape
    q = (rms_norm(c_q_raw, q_lat_g) @ w_uq).reshape(b, s, N_HEADS, QK_DIM)
    kv = (rms_norm(c_kv_raw, kv_lat_g) @ w_ukv).reshape(b, s, N_HEADS, QK_NOPE + V_DIM)
    k_nope, v = kv[..., :QK_NOPE], kv[..., QK_NOPE:]
    k_pe = jnp.broadcast_to(k_pe_raw[:, :, None, :], (b, s, N_HEADS, QK_ROPE))
    k = jnp.concatenate([k_nope, k_pe], axis=-1)
    q = rms_norm(q, q_g)
    k = rms_norm(k, k_g)
    q = jnp.concatenate([q[..., :QK_NOPE], apply_rope(q[..., QK_NOPE:], cos, sin)], axis=-1)
    k = jnp.concatenate([k[..., :QK_NOPE], apply_rope(k[..., QK_NOPE:], cos, sin)], axis=-1)
    q = jnp.transpose(q, (0, 2, 1, 3))
    k = jnp.transpose(k, (0, 2, 1, 3))
    v = jnp.transpose(v, (0, 2, 1, 3))
    return chunk_causal_attention(q, k, v)


def conv_group(u_raw, conv_w, conv_b, ln_g, ln_b):
    a, g = u_raw[..., :CONV_W], u_raw[..., CONV_W:]
    u = a * jax.nn.sigmoid(g)
    y = lax.conv_general_dilated(
        u, conv_w[:, None, :].astype(u.dtype),
        window_strides=(1,), padding=[(CONV_K - 1, 0)],
        dimension_numbers=('NWC', 'WIO', 'NWC'),
        feature_group_count=CONV_W)
    y = y + conv_b
    return jax.nn.silu(layer_norm(y, ln_g, ln_b))


def setup_inputs(seed: int = 0) -> dict:
    key = jax.random.key(seed)
    ks = jax.random.split(key, 24)
    L, D, F = DEPTH, D_MODEL, D_FF

    def w(k, shape, fan_in):
        return jax.random.normal(k, shape, jnp.float32) * (fan_in ** -0.5)

    def gain(k, shape):
        return 1.0 + 0.02 * jax.random.normal(k, shape, jnp.float32)

    def bias(k, shape):
        return 0.02 * jax.random.normal(k, shape, jnp.float32)

    return {
        "x": jax.random.normal(ks[0], (BATCH, SEQ, D), jnp.float32),
        "ffn1_norm": gain(ks[1], (L, D)),
        "ffn1_w_gate": w(ks[2], (L, D, F), D),
        "ffn1_w_up": w(ks[3], (L, D, F), D),
        "ffn1_w_down": w(ks[4], (L, F, D), F),
        "mix_norm": gain(ks[5], (L, D)),
        "w_in": w(ks[6], (L, D, IN_COLS), D),
        "q_latent_norm": gain(ks[7], (L, Q_LORA)),
        "w_uq": w(ks[8], (L, Q_LORA, N_HEADS * QK_DIM), Q_LORA),
        "kv_latent_norm": gain(ks[9], (L, KV_LORA)),
        "w_ukv": w(ks[10], (L, KV_LORA, N_HEADS * (QK_NOPE + V_DIM)), KV_LORA),
        "q_norm": gain(ks[11], (L, QK_DIM)),
        "k_norm": gain(ks[12], (L, QK_DIM)),
        "conv_w": w(ks[13], (L, CONV_K, CONV_W), CONV_K),
        "conv_b": bias(ks[14], (L, CONV_W)),
        "conv_ln_g": gain(ks[15], (L, CONV_W)),
        "conv_ln_b": bias(ks[16], (L, CONV_W)),
        "w_out": w(ks[17], (L, D_MIX, D), D_MIX),
        "ffn2_norm": gain(ks[18], (L, D)),
        "ffn2_w_gate": w(ks[19], (L, D, F), D),
        "ffn2_w_up": w(ks[20], (L, D, F), D),
        "ffn2_w_down": w(ks[21], (L, F, D), F),
        "post_norm": gain(ks[22], (L, D)),
    }


def reference(x, ffn1_norm, ffn1_w_gate, ffn1_w_up, ffn1_w_down, mix_norm, w_in,
              q_latent_norm, w_uq, kv_latent_norm, w_ukv, q_norm, k_norm,
              conv_w, conv_b, conv_ln_g, conv_ln_b, w_out,
              ffn2_norm, ffn2_w_gate, ffn2_w_up, ffn2_w_down, post_norm):
    cos, sin = rope_tables(x.shape[1])
    o_kv = Q_LORA
    o_pe = Q_LORA + KV_LORA
    o_cv = Q_LORA + KV_LORA + QK_ROPE
    for l in range(DEPTH):
        x = x + 0.5 * swiglu(rms_norm(x, ffn1_norm[l]), ffn1_w_gate[l], ffn1_w_up[l], ffn1_w_down[l])
        h = rms_norm(x, mix_norm[l])
        p = h @ w_in[l]
        attn = mla_group(p[..., :o_kv], p[..., o_kv:o_pe], p[..., o_pe:o_cv],
                         q_latent_norm[l], w_uq[l], kv_latent_norm[l], w_ukv[l],
                         q_norm[l], k_norm[l], cos, sin)
        conv = conv_group(p[..., o_cv:], conv_w[l], conv_b[l], conv_ln_g[l], conv_ln_b[l])
        x = x + jnp.concatenate([attn, conv], axis=-1) @ w_out[l]
        x = x + 0.5 * swiglu(rms_norm(x, ffn2_norm[l]), ffn2_w_gate[l], ffn2_w_up[l], ffn2_w_down[l])
        x = rms_norm(x, post_norm[l])
    return x
```

```python
import numpy as np
import concourse.bass as bass
import concourse.mybir as mybir
from concourse.bass_utils import run_bass_kernel_spmd

F32 = mybir.dt.float32
BF16 = mybir.dt.bfloat16
AF = mybir.ActivationFunctionType
ALU = mybir.AluOpType
AX = mybir.AxisListType

L = 2
D = 1024
FF = 2816
SEQ = 4096
TB = 512
NH = 8
EPS = 1e-6
NF = FF // 128
NSLAB = FF // 256
NVL = 176
WSLOT = 4096

ENGS = ("pe", "act", "dve", "pool", "sp")


class Op:
    __slots__ = ("eng", "fn", "reads", "writes", "dma", "key", "idx", "deps", "signal", "seq", "nwait")

    def __init__(self, eng, fn, reads, writes, dma, key):
        self.eng, self.fn, self.reads, self.writes = eng, fn, reads, writes
        self.dma, self.key = dma, key
        self.deps = []
        self.signal = False
        self.seq = 0


class Sched:
    def __init__(self, nc):
        self.nc = nc
        self.ops = []
        self.lastw = {}
        self.readers = {}
        self.dma_cnt = {}

    def op(self, eng, fn, reads=(), writes=(), dma=False, key=None):
        psr = [r for r in reads if isinstance(r, tuple) and r[0] == "ps"]
        if psr:
            reads = [r for r in reads if not (isinstance(r, tuple) and r[0] == "ps")]
            writes = list(writes) + [r for r in psr if r not in writes]
        o = Op(eng, fn, tuple(reads), tuple(writes), dma, key)
        o.idx = len(self.ops)
        deps = set()
        for r in o.reads:
            w = self.lastw.get(r)
            if w is not None:
                deps.add((w, 0))
        for r in o.writes:
            w = self.lastw.get(r)
            if w is not None:
                deps.add((w, 1))
            for rd in self.readers.get(r, ()):
                deps.add((rd, 2))
        for r in o.reads:
            self.readers.setdefault(r, []).append(o.idx)
        for r in o.writes:
            self.lastw[r] = o.idx
            self.readers[r] = []
        dd = {}
        for (d, kind) in deps:
            if d == o.idx:
                continue
            dop = self.ops[d]
            if (not dop.dma) and (not dma) and dop.eng == eng and (kind != 0 or eng == "pe"):
                continue
            dd[d] = True
        o.deps = sorted(dd.keys())
        o.nwait = {}
        for d in o.deps:
            dop = self.ops[d]
            if dop.dma:
                o.nwait[dop.key] = self.dma_cnt[dop.key]
            else:
                dop.signal = True
        if dma:
            assert key is not None
            self.dma_cnt[key] = self.dma_cnt.get(key, 0) + 1
        self.ops.append(o)
        return o

    def emit(self):
        nc = self.nc
        cnt = {e: 0 for e in ENGS}
        for o in self.ops:
            if not o.dma and o.signal:
                cnt[o.eng] += 1
                o.seq = cnt[o.eng]
        esem = {e: nc.alloc_semaphore(f"s_{e}") for e in ENGS}
        ksem = {k: nc.alloc_semaphore(f"d_{i}") for i, k in enumerate(self.dma_cnt)}
        per = {e: [o for o in self.ops if o.eng == e] for e in ENGS}
        ops = self.ops

        def run(eng_name, h):
            known = {}
            for o in per[eng_name]:
                waits = {}
                for d in o.deps:
                    dop = ops[d]
                    if dop.dma:
                        s, v = ksem[dop.key], 16 * o.nwait[dop.key]
                    else:
                        s, v = esem[dop.eng], dop.seq
                    if waits.get(id(s), (None, 0))[1] < v:
                        waits[id(s)] = (s, v)
                for s, v in waits.values():
                    if known.get(id(s), 0) < v:
                        h.wait_ge(s, v)
                        known[id(s)] = v
                ins = o.fn(h)
                if o.dma:
                    ins.then_inc(ksem[o.key], 16)
                elif o.signal:
                    ins.then_inc(esem[eng_name], 1)

        with nc.Block() as block:
            @block.tensor
            def _(h):
                run("pe", h)

            @block.scalar
            def _(h):
                run("act", h)

            @block.vector
            def _(h):
                run("dve", h)

            @block.gpsimd
            def _(h):
                run("pool", h)

            @block.sync
            def _(h):
                run("sp", h)


def vec_cols(l):
    b = l * NVL
    o = {}
    o["f1n"] = b; o["mixn"] = b + 8; o["f2n"] = b + 16; o["postn"] = b + 24
    o["qln"] = b + 32; o["kvln"] = b + 35
    o["gq"] = b + 37; o["gkA"] = b + 38; o["gkpe"] = b + 39
    o["cb"] = b + 40; o["clg"] = b + 44; o["clb"] = b + 48; o["cw"] = b + 52
    return o


def build_nc(nblk=8, nlayers=2, dbg=None):
    nc = bass.Bass("TRN2", target_bir_lowering=False)
    S = Sched(nc)
    T = nblk * TB

    def din(name, shape, dt=F32):
        return nc.dram_tensor(name, list(shape), dt, kind="ExternalInput").ap()

    xT_in = din("xT", [D, T])
    outT = nc.dram_tensor("outT", [D, T], F32, kind="ExternalOutput").ap()
    gu_in = [din("ffn1_gu", [L, NSLAB, 256, 2048]), din("ffn2_gu", [L, NSLAB, 256, 2048])]
    wd_in = [din("ffn1_wd", [L, 8, 176, 2048]), din("ffn2_wd", [L, 8, 176, 2048])]
    win_in = [din("win0", [L, 192, 2048]), din("win1", [L, 160, 2048]), din("win2", [L, 256, 2048]),
              din("win3", [L, 256, 2048])]
    wuq_in = din("wuq", [L, 192, 2048])
    wkv_in = din("wkv", [L, 128, 2048])
    wo_in = din("wo", [L, 2, 256, 2048])
    vecs_d = din("vecs", [128, L * NVL])
    qk_d = din("qk_bc", [128, L * 2 * 96])
    rope_d = din("ropeT", [128, T])

    def dscr(name, shape, dt=BF16):
        return nc.dram_tensor(name, list(shape), dt, kind="Internal").ap()

    gu_s = [[dscr(f"gu_s{t}_{l}", [NSLAB, 128, 2, 8, 256]) for l in range(L)] for t in range(2)]
    wd_s = [[dscr(f"wd_s{t}_{l}", [8, 128, NF, 128]) for l in range(L)] for t in range(2)]
    win_s = [[dscr(f"win_s{l}_0", [128, 8, 384]), dscr(f"win_s{l}_1", [128, 8, 320]),
              dscr(f"win_s{l}_2", [128, 8, 512]), dscr(f"win_s{l}_3", [128, 8, 512])] for l in range(L)]
    wuq_s = [dscr(f"wuq_s{l}", [128, 3, 1024]) for l in range(L)]
    wkv_s = [dscr(f"wkv_s{l}", [128, 2, 2, 512]) for l in range(L)]
    wo_s = [dscr(f"wo_s{l}", [2, 128, 8, 512]) for l in range(L)]
    k_s = [dscr(f"k_s{l}", [NH, 96, T]) for l in range(L)]
    v_s = [dscr(f"v_s{l}", [NH, 128, 4 * nblk, 128]) for l in range(L)]

    sb = nc.alloc_sbuf_tensor
    xT = sb("xTs", [128, 8, TB], F32)
    hT = sb("hT", [128, 8, TB], BF16)
    gT = sb("gT", [128, NF, TB], BF16)
    NW = 5
    wr = [sb(f"wr{i}", [128, WSLOT], BF16) for i in range(NW)]
    vecs = sb("vecs_sb", [128, L * NVL], F32)
    qkb = sb("qkb", [128, L * 2, 96], F32)
    qkm = sb("qkm", [128, L * 2], F32)
    negsh = sb("negsh", [128, L], F32)
    epsT = sb("epsT", [128, 1], F32)
    ones = sb("ones", [128, 128], BF16)
    bones = sb("bones", [128, 128], BF16)
    ropeT = sb("ropeTs", [128, TB], F32)
    rstd = [sb(f"rstd{i}", [128, TB], F32) for i in range(2)]
    lnv = [sb(f"lnv{i}", [128, TB], F32) for i in range(2)]
    sil = [sb(f"sil{i}", [128, TB], F32) for i in range(2)]
    lat = sb("lat", [128, 5, TB], F32)
    sql = sb("sql", [128, 5, TB], BF16)
    latn = sb("latn", [128, 5, TB], BF16)
    sqpe = sb("sqpe", [32, TB], BF16)
    kmisc = sb("kmisc", [128, 2, TB], F32)
    ktmp = kmisc[0:32]
    kr = kmisc[:, 0, :]
    rsA = kmisc[:, 1, :]
    rmix = sb("rmix", [128, TB], F32)
    rF2 = sb("rF2", [128, TB], F32)
    gpg = sb("gpg", [128, 8 * L], F32)
    sqq = [sb(f"sqq{i}", [128, TB], BF16) for i in range(2)]
    qs = [sb(f"qs{i}", [128, TB], F32) for i in range(1)]
    rt = [sb(f"rt{i}", [128, 2, TB], F32) for i in range(1)]
    Qt = sb("Qt", [128, NH, TB], BF16)
    Kt = sb("Kt", [128, NH, TB], BF16)
    Va = sb("Va", [128, NH, 4, 128], BF16)
    NKV = 6
    kvr = [sb(f"kvr{i}", [128, 1024], BF16) for i in range(NKV)]
    NPT = 4
    PtA = sb("PtA", [128, NPT * TB], BF16)
    Pt = [PtA[:, i * TB:(i + 1) * TB] for i in range(NPT)]
    lns = [sb(f"lns{i}", [128, TB], F32) for i in range(1)]
    rinv = [sb(f"rinv{i}", [128, TB], F32) for i in range(1)]
    mixT = sb("mixT", [128, 8, TB], BF16)
    ub = [sb(f"ub{l}", [128, 4, TB + 30], BF16) for l in range(L)]
    cmean = sb("cmean", [128, TB], F32)
    ct = [sb(f"ct{i}", [128, TB], F32) for i in range(2)]
    psA = nc.alloc_psum_tensor("psA", [128, 8 * TB], F32)
    ps = [psA[:, i * TB:(i + 1) * TB] for i in range(8)]
    print("sbuf bytes remaining", nc.sbuf_bytes_remaining)

    ctr = {}

    def rr(name, n):
        v = ctr.get(name, 0)
        ctr[name] = v + 1
        return v % n

    pools = {"g": [0, 1], "u": [2, 3], "o": [4, 5], "ss": [6, 7], "p4": [0, 1, 2, 3]}

    def bank(pool):
        lst = pools[pool]
        return lst[rr("ps_" + pool, len(lst))]

    def PS(b):
        return ("ps", b)

    def load_w(src_ap, n_el, reads, view=None):
        s = rr("wr", NW)
        dst = wr[s][:, 0:n_el]
        S.op("sp", lambda h: h.dma_start(out=dst, in_=src_ap), reads=reads, writes=[("wr", s)],
             dma=True, key=("wr", s))
        return s

    def mm_group(out_ap, pairs, reads, bnk):
        n = len(pairs)

        def fn(h):
            ins = None
            for i, (a, b) in enumerate(pairs):
                ins = h.matmul(out_ap, lhsT=a, rhs=b, start=(i == 0), stop=(i == n - 1))
            return ins
        S.op("pe", fn, reads=reads, writes=[PS(bnk)])

    def act(out, in_, func, reads, writes, scale=None, bias=None):
        kw = {}
        if scale is not None:
            kw["scale"] = scale
        if bias is not None:
            kw["bias"] = bias
        S.op("act", lambda h: h.activation(out=out, in_=in_, func=func, **kw), reads=reads, writes=writes)

    def stt(out, in0, scalar, in1, op0, op1, reads, writes):
        S.op("dve", lambda h: h.scalar_tensor_tensor(out=out, in0=in0, scalar=scalar, in1=in1, op0=op0, op1=op1),
             reads=reads, writes=writes)

    def tt(eng, out, in0, in1, op, reads, writes):
        S.op(eng, lambda h: h.tensor_tensor(out=out, in0=in0, in1=in1, op=op), reads=reads, writes=writes)

    XT = [("xT", c) for c in range(8)]
    HT = [("hT", c) for c in range(8)]

    def rstd_from(bnk, nparts, inv_n, reads_extra=()):
        i = rr("rstd", 2)
        act(lnv[i][0:nparts, :], ps[bnk][0:nparts, :], AF.Ln, reads=[PS(bnk), "epsT"], writes=[("lnv", i)],
            scale=inv_n, bias=epsT[0:nparts, :])
        act(rstd[i][0:nparts, :], lnv[i][0:nparts, :], AF.Exp, reads=[("lnv", i)], writes=[("rstd", i)], scale=-0.5)
        return i

    def ss_open(buf, nm):
        return dict(b=bank("ss"), n=0, pend=None, buf=buf, nm=nm)

    def ss_mm(st, c):
        b, buf, nm = st["b"], st["buf"], st["nm"]
        first, last = (st["n"] == 0), (st["n"] == 7)
        S.op("pe", lambda h: h.matmul(ps[b][:], lhsT=ones[:], rhs=buf[:, c, :], start=first, stop=last),
             reads=[(nm, c), "ones"], writes=[PS(b)])
        st["n"] += 1

    def sq_acc(st, c, scale=None):
        act(st["buf"][:, c, :], xT[:, c, :], AF.Square, reads=[("xT", c), "vecs"], writes=[(st["nm"], c)],
            scale=scale)
        if st["pend"] is not None:
            ss_mm(st, st["pend"])
        st["pend"] = c

    def ss_close(st):
        ss_mm(st, st["pend"])
        st["pend"] = None
        assert st["n"] == 8

    def norm_finish(st, gcol, inplace=False, hook=None):
        ss_close(st)
        i = rstd_from(st["b"], 128, 1.0 / D)
        for c in range(8):
            dst, nm = (xT, "xT") if inplace else (hT, "hT")
            stt(dst[:, c, :], xT[:, c, :], vecs[:, gcol + c:gcol + c + 1], rstd[i][:], ALU.mult, ALU.mult,
                reads=[("xT", c), ("rstd", i), "vecs"], writes=[(nm, c)])
            if hook is not None:
                hook(c)

    def norm_block_start(gcol):
        st = ss_open(hT, "hT")
        act(hT[:], xT[:], AF.Square, reads=XT, writes=HT)
        for c in range(7):
            ss_mm(st, c)
        st["pend"] = 7
        norm_finish(st, gcol)

    S.op("sp", lambda h: h.dma_start(out=vecs[:], in_=vecs_d), writes=["vecs"], dma=True, key="setup")
    S.op("sp", lambda h: h.dma_start(out=qkb[:], in_=qk_d.rearrange("p (a b) -> p a b", b=96)), writes=["qkb"],
         dma=True, key="setup")
    S.op("dve", lambda h: h.memset(epsT[:], EPS), writes=["epsT"])
    S.op("dve", lambda h: h.memset(ones[:], 1.0), writes=["ones"])
    S.op("dve", lambda h: h.memset(bones[:], 0.0), writes=["bones"])
    S.op("dve", lambda h: h.memset(bones[0:64, 0:64], 1.0), writes=["bones"])
    S.op("dve", lambda h: h.memset(bones[64:128, 64:128], 1.0), writes=["bones"])
    S.op("dve", lambda h: h.memset(Va[:], 1.0), writes=["Va"])
    for l in range(L):
        S.op("dve", lambda h, l=l: h.memset(ub[l][:, :, 0:30], 0.0), writes=[("ub", l)])
    S.op("dve", lambda h: h.tensor_reduce(out=qkm[:], in_=qkb[:], axis=AX.X, op=ALU.max, apply_absolute_value=True),
         reads=["qkb"], writes=["qkm"])
    qkm_v = qkm[:].rearrange("p (l t) -> p l t", t=2)
    S.op("dve", lambda h: h.tensor_tensor(out=negsh[:], in0=qkm_v[:, :, 0], in1=qkm_v[:, :, 1], op=ALU.mult),
         reads=["qkm"], writes=["negsh0"])
    S.op("dve", lambda h: h.tensor_scalar(out=negsh[:], in0=negsh[:], scalar1=-float(np.sqrt(96.0)), scalar2=None,
                                           op0=ALU.mult), reads=["negsh0"], writes=["negsh"])

    def cast(dst, src, wreg, key):
        S.op("pool", lambda h: h.dma_start(out=dst, in_=src), writes=[wreg], dma=True, key=key)

    def flat(ap_):
        names = " ".join(f"a{i}" for i in range(len(ap_.shape)))
        return ap_.rearrange(f"{names} -> ({names})").rearrange("(r n) -> r n", n=2048)

    def conv_ffn(t, l):
        for s in range(NSLAB):
            cast(flat(gu_s[t][l][s]), gu_in[t][l][s], ("Wgu", t, l, s), ("cgu", t, l, s if (t == 0 and l == 0) else 0))
        for d in range(8):
            cast(flat(wd_s[t][l][d]), wd_in[t][l][d], ("Wd", t, l, d), ("cwd", t, l, d if (t == 0 and l == 0) else 0))

    def conv_mix(l):
        for i in range(4):
            cast(flat(win_s[l][i]), win_in[i][l], ("Win", l, i), ("cmix", l))
        cast(flat(wuq_s[l]), wuq_in[l], ("Wuq", l), ("cmix", l))
        cast(flat(wkv_s[l]), wkv_in[l], ("Wkv", l), ("cmix", l))
        for s in range(2):
            cast(flat(wo_s[l][s]), wo_in[l][s], ("Wo", l, s), ("cmix", l))

    for l in range(nlayers):
        conv_ffn(0, l)
        conv_mix(l)
        conv_ffn(1, l)

    def hprep(d, gap, extra="vecs"):
        S.op("dve", lambda h: h.tensor_scalar(out=hT[:, d, :], in0=xT[:, d, :], scalar1=gap, scalar2=None,
                                               op0=ALU.mult),
             reads=[("xT", d), extra], writes=[("hT", d)])

    def r_from(st, rb):
        ss_close(st)
        act(lnv[0][:], ps[st["b"]][:], AF.Ln, reads=[PS(st["b"]), "epsT"], writes=[("lnv", 0)],
            scale=1.0 / D, bias=epsT[:])
        act(rb[0][:], lnv[0][:], AF.Exp, reads=[("lnv", 0)], writes=[rb[1]], scale=-0.5)

    def ffn(t, l, vc, rb, hook=None):
        rbuf, rname = rb
        for s in range(NSLAB):
            sl = load_w(gu_s[t][l][s].rearrange("p a c f -> p (a c f)"), 4096,
                        reads=[("Wgu", t, l, s)])
            wv = wr[sl][:].rearrange("p (a c f) -> p a c f", a=2, c=8)
            for ff in range(2):
                f = 2 * s + ff
                bg = bank("g")
                mm_group(ps[bg][:], [(wv[:, 0, c, ff * 128:(ff + 1) * 128], hT[:, c, :]) for c in range(8)],
                         reads=HT + [("wr", sl)], bnk=bg)
                bu = bank("u")
                mm_group(ps[bu][:], [(wv[:, 1, c, ff * 128:(ff + 1) * 128], hT[:, c, :]) for c in range(8)],
                         reads=HT + [("wr", sl)], bnk=bu)
                i = rr("sil", 2)
                tt("dve", sil[i][:], ps[bg][:], rbuf[:], ALU.mult, reads=[PS(bg), rname], writes=[("sil", i)])
                act(sil[i][:], sil[i][:], AF.Silu, reads=[("sil", i)], writes=[("sil", i)])
                tt("dve", gT[:, f, :], ps[bu][:], sil[i][:], ALU.mult, reads=[PS(bu), ("sil", i)], writes=[("gT", f)])
        for d in range(8):
            sl = load_w(wd_s[t][l][d].rearrange("p f n -> p (f n)"), NF * 128, reads=[("Wd", t, l, d)])
            wv = wr[sl][:, 0:NF * 128].rearrange("p (f n) -> p f n", n=128)
            bo = bank("o")
            mm_group(ps[bo][:], [(wv[:, f, :], gT[:, f, :]) for f in range(NF)],
                     reads=[("gT", f) for f in range(NF)] + [("wr", sl)], bnk=bo)
            ci = rr("ct", 2)
            tt("dve", ct[ci][:], ps[bo][:], rbuf[:], ALU.mult, reads=[PS(bo), rname], writes=[("ct", ci)])
            stt(xT[:, d, :], ct[ci][:], 0.5, xT[:, d, :], ALU.mult, ALU.add, reads=[("ct", ci), ("xT", d)],
                writes=[("xT", d)])
            if hook is not None:
                hook(d)

    def mixer(l, j, vc, stm, hook2):
        ss_close(stm)
        act(lnv[0][:], ps[stm["b"]][:], AF.Ln, reads=[PS(stm["b"]), "epsT", ("lnv", 0)], writes=[("lnv", 0)],
            scale=1.0 / D, bias=epsT[:])
        act(rmix[:], lnv[0][:], AF.Exp, reads=[("lnv", 0)], writes=["rmix"], scale=-0.5)
        if j > 0:
            S.op("pool", lambda h: h.tensor_copy(out=ub[l][:, :, 0:30], in_=ub[l][:, :, TB:TB + 30]),
                 reads=[("ub", l)], writes=[("ub", l)])
        s0 = load_w(win_s[l][0].rearrange("p c n -> p (c n)"), 8 * 384, reads=[("Win", l, 0)])
        w0 = wr[s0][:, 0:8 * 384].rearrange("p (c n) -> p c n", n=384)
        s1 = load_w(win_s[l][1].rearrange("p c n -> p (c n)"), 8 * 320, reads=[("Win", l, 1)])
        w1 = wr[s1][:, 0:8 * 320].rearrange("p (c n) -> p c n", n=320)
        for i in range(5):
            b = bank("p4")
            if i < 3:
                pairs = [(w0[:, c, i * 128:(i + 1) * 128], hT[:, c, :]) for c in range(8)]
                rd = HT + [("wr", s0)]
            else:
                pairs = [(w1[:, c, (i - 3) * 128:(i - 2) * 128], hT[:, c, :]) for c in range(8)]
                rd = HT + [("wr", s1)]
            mm_group(ps[b][:], pairs, reads=rd, bnk=b)
            tt("dve", lat[:, i, :], ps[b][:], rmix[:], ALU.mult, reads=[PS(b), "rmix"], writes=[("lat", i)])
            act(sql[:, i, :], lat[:, i, :], AF.Square, reads=[("lat", i)], writes=[("sql", i)])
        bk = bank("p4")
        mm_group(ps[bk][0:64, :], [(w1[:, c, 256:320], hT[:, c, :]) for c in range(8)], reads=HT + [("wr", s1)], bnk=bk)
        tt("dve", qs[0][0:64, :], ps[bk][0:64, :], rmix[0:64, :], ALU.mult, reads=[PS(bk), "rmix"], writes=[("qs", 0)])
        act(sqpe[:], qs[0][0:32, :], AF.Square, reads=[("qs", 0)], writes=["sqpe"])
        g0 = vc["gkpe"]
        stt(ktmp[:, 0, :], qs[0][0:32, :], vecs[0:32, g0:g0 + 1], ropeT[0:32, :], ALU.mult, ALU.mult,
            reads=[("qs", 0), "vecs", "ropeT"], writes=[("ktmp", 0)])
        stt(ktmp[:, 1, :], qs[0][32:64, :], vecs[32:64, g0:g0 + 1], ropeT[32:64, :], ALU.mult, ALU.mult,
            reads=[("qs", 0), "vecs", "ropeT"], writes=[("ktmp", 1)])
        tt("pool", kr[64:96, :], ktmp[:, 0, :], ktmp[:, 1, :], ALU.add, reads=[("ktmp", 0), ("ktmp", 1)], writes=["kr"])
        for (lo, n, gname, inv) in ((0, 3, "qln", 1.0 / 384), (3, 2, "kvln", 1.0 / 256)):
            b = bank("ss")
            mm_group(ps[b][:], [(ones[:], sql[:, lo + i, :]) for i in range(n)],
                     reads=[("sql", lo + i) for i in range(n)] + ["ones"], bnk=b)
            ri = rstd_from(b, 128, inv)
            for i in range(n):
                col = vc[gname] + i
                stt(latn[:, lo + i, :], lat[:, lo + i, :], vecs[:, col:col + 1], rstd[ri][:], ALU.mult, ALU.mult,
                    reads=[("lat", lo + i), ("rstd", ri), "vecs"], writes=[("latn", lo + i)])
        s2 = load_w(win_s[l][2].rearrange("p c n -> p (c n)"), 4096, reads=[("Win", l, 2)])
        w2 = wr[s2][:].rearrange("p (c n) -> p c n", n=512)
        s3 = load_w(win_s[l][3].rearrange("p c n -> p (c n)"), 4096, reads=[("Win", l, 3)])
        w3 = wr[s3][:].rearrange("p (c n) -> p c n", n=512)
        for cc in range(4):
            ba = bank("g")
            mm_group(ps[ba][:], [(w2[:, c, cc * 128:(cc + 1) * 128], hT[:, c, :]) for c in range(8)],
                     reads=HT + [("wr", s2)], bnk=ba)
            bg = bank("u")
            mm_group(ps[bg][:], [(w3[:, c, cc * 128:(cc + 1) * 128], hT[:, c, :]) for c in range(8)],
                     reads=HT + [("wr", s3)], bnk=bg)
            i = rr("sil", 2)
            tt("dve", sil[i][:], ps[bg][:], rmix[:], ALU.mult, reads=[PS(bg), "rmix"], writes=[("sil", i)])
            act(sil[i][:], sil[i][:], AF.Sigmoid, reads=[("sil", i)], writes=[("sil", i)])
            ci = rr("ct", 2)
            tt("dve", ct[ci][:], ps[ba][:], sil[i][:], ALU.mult, reads=[PS(ba), ("sil", i)], writes=[("ct", ci)])
            tt("pool", ub[l][:, cc, 30:30 + TB], ct[ci][:], rmix[:], ALU.mult, reads=[("ct", ci), "rmix"],
               writes=[("ub", l)])
        sq_ = load_w(wuq_s[l].rearrange("p c n -> p (c n)"), 3 * 1024, reads=[("Wuq", l)])
        wq = wr[sq_][:, 0:3 * 1024].rearrange("p (c n) -> p c n", n=1024)
        skv = load_w(wkv_s[l].rearrange("p a c n -> p (a c n)"), 2048, reads=[("Wkv", l)])
        wkv = wr[skv][:, 0:2048].rearrange("p (a c n) -> p a c n", a=2, c=2)
        gq = vc["gq"]
        gk = vc["gkA"]

        def qA(h_):
            b = bank("p4")
            mm_group(ps[b][:], [(wq[:, c, h_ * 128:(h_ + 1) * 128], latn[:, c, :]) for c in range(3)],
                     reads=[("latn", c) for c in range(3)] + [("wr", sq_)], bnk=b)
            qi = rr("sqq", 2)
            act(sqq[qi][0:96, :], ps[b][0:96, :], AF.Square, reads=[PS(b)], writes=[("sqq", qi)])
            return (b, qi)

        def qB(h_, st_):
            b, qi = st_
            bs = bank("ss")
            mm_group(ps[bs][:], [(ones[0:96, :], sqq[qi][0:96, :])], reads=[("sqq", qi), "ones"], bnk=bs)
            ri = rstd_from(bs, 128, 1.0 / 96)
            stt(Qt[0:64, h_, :], ps[b][0:64, :], vecs[0:64, gq:gq + 1], rstd[ri][0:64, :], ALU.mult, ALU.mult,
                reads=[PS(b), ("rstd", ri), "vecs"], writes=[("Qt", h_)])
            si = rr("qs", 1)
            stt(qs[si][64:128, :], ps[b][64:128, :], vecs[64:128, gq:gq + 1], rstd[ri][64:128, :], ALU.mult, ALU.mult,
                reads=[PS(b), ("rstd", ri), "vecs"], writes=[("qs", si)])
            ti = rr("rt", 1)
            tt("dve", rt[ti][64:96, 0, :], qs[si][64:96, :], ropeT[64:96, :], ALU.mult,
               reads=[("qs", si), "ropeT"], writes=[("rt", ti, 0)])
            tt("pool", rt[ti][64:96, 1, :], qs[si][96:128, :], ropeT[96:128, :], ALU.mult,
               reads=[("qs", si), "ropeT"], writes=[("rt", ti, 1)])
            tt("pool", Qt[64:96, h_, :], rt[ti][64:96, 0, :], rt[ti][64:96, 1, :], ALU.add,
               reads=[("rt", ti, 0), ("rt", ti, 1)], writes=[("Qt", h_)])

        def kA(pi):
            b = bank("p4")
            mm_group(ps[b][:], [(wkv[:, 0, c, pi * 128:(pi + 1) * 128], latn[:, 3 + c, :]) for c in range(2)],
                     reads=[("latn", 3), ("latn", 4), ("wr", skv)], bnk=b)
            qi = rr("sqq", 2)
            act(sqq[qi][:], ps[b][:], AF.Square, reads=[PS(b)], writes=[("sqq", qi)])
            return (b, qi)

        def kB(pi, st_):
            b, qi = st_
            bs = bank("ss")
            mm_group(ps[bs][:], [(bones[:], sqq[qi][:]), (ones[0:32, :], sqpe[:])],
                     reads=[("sqq", qi), "sqpe", "ones", "bones"], bnk=bs)
            ri = rstd_from(bs, 128, 1.0 / 96)
            act(rsA[64:96, :], lnv[ri][0:32, :], AF.Exp, reads=[("lnv", ri)], writes=["rsA"], scale=-0.5)
            hA, hB = 2 * pi, 2 * pi + 1
            stt(Kt[0:64, hA, :], ps[b][0:64, :], vecs[0:64, gk:gk + 1], rstd[ri][0:64, :], ALU.mult, ALU.mult,
                reads=[PS(b), ("rstd", ri), "vecs"], writes=[("Kt", hA)])
            stt(Kt[0:64, hB, :], ps[b][64:128, :], vecs[64:128, gk:gk + 1], rstd[ri][64:128, :], ALU.mult, ALU.mult,
                reads=[PS(b), ("rstd", ri), "vecs"], writes=[("Kt", hB)])
            tt("pool", Kt[64:96, hA, :], kr[64:96, :], rsA[64:96, :], ALU.mult, reads=["kr", "rsA"], writes=[("Kt", hA)])
            tt("pool", Kt[64:96, hB, :], kr[64:96, :], rstd[ri][64:96, :], ALU.mult, reads=["kr", ("rstd", ri)],
               writes=[("Kt", hB)])

        def vA(t_):
            b = bank("p4")
            mm_group(ps[b][:], [(latn[:, 3 + c, t_ * 128:(t_ + 1) * 128], wkv[:, 1, c, :]) for c in range(2)],
                     reads=[("latn", 3), ("latn", 4), ("wr", skv)], bnk=b)
            return b

        def vB(t_, b):
            pv = ps[b][:].rearrange("p (h d) -> p h d", d=64)
            act(Va[:, 0:8:2, t_, 0:64], pv[:, 0:8:2, :], AF.Copy, reads=[PS(b)], writes=["Va"])
            S.op("dve", lambda h, t_=t_, pv=pv: h.tensor_copy(out=Va[:, 1:8:2, t_, 64:128], in_=pv[:, 1:8:2, :]),
                 reads=[PS(b)], writes=["Va"])

        stages = []
        for g_ in range(4):
            stages += [(qA, qB, 2 * g_), (qA, qB, 2 * g_ + 1), (kA, kB, g_), (vA, vB, g_)]
        prev = None
        for (fa, fb, arg) in stages:
            cur = (fb, arg, fa(arg))
            if prev is not None:
                prev[0](prev[1], prev[2])
            prev = cur
        prev[0](prev[1], prev[2])
        if j < nblk - 1:
            S.op("pool", lambda h: h.dma_start(out=k_s[l][:, :, j * TB:(j + 1) * TB].rearrange("h p t -> p h t"),
                                              in_=Kt[0:96, :, :]),
                 reads=[("Kt", h_) for h_ in range(NH)], writes=[("Ks", l, j)], dma=True, key=("kvs", l, j))
            S.op("pool", lambda h: h.dma_start(out=v_s[l][:, :, 4 * j:4 * j + 4, :].rearrange("h p t n -> p h t n"),
                                              in_=Va[:]),
                 reads=["Va"], writes=[("Vs", l, j)], dma=True, key=("kvs", l, j))
        def conv_taps(k_lo, k_hi):
            for k in range(k_lo, k_hi):
                for cc in range(4):
                    wc = vc["cw"] + cc * 31 + k
                    if k == 0:
                        cbc = vc["cb"] + cc
                        S.op("dve", lambda h, cc=cc, wc=wc, cbc=cbc: h.tensor_scalar(
                            out=lat[:, cc, :], in0=ub[l][:, cc, 0:TB], scalar1=vecs[:, wc:wc + 1],
                            scalar2=vecs[:, cbc:cbc + 1], op0=ALU.mult, op1=ALU.add),
                            reads=[("ub", l), "vecs"], writes=[("lat", cc)])
                    else:
                        stt(lat[:, cc, :], ub[l][:, cc, k:k + TB], vecs[:, wc:wc + 1], lat[:, cc, :], ALU.mult, ALU.add,
                            reads=[("ub", l), "vecs", ("lat", cc)], writes=[("lat", cc)])
        conv_taps(0, 4)
        sc = float(96.0 ** -0.5)
        LA = 2
        tiles = []
        for h_ in range(NH):
            ntile = 4 * (j + 1)
            tcount = 0
            for i in range(j + 1):
                for t_ in range(4):
                    tiles.append(dict(h=h_, i=i, t=t_, n0=(t_ * 128 if i == j else 0), first=(tcount == 0),
                                      last=(tcount == ntile - 1)))
                    tcount += 1
        hstate = {}
        units = []
        k_ = 0
        while k_ < len(tiles):
            tl = tiles[k_]
            if tl["i"] < j and tl["t"] % 2 == 0:
                units.append([tiles[k_], tiles[k_ + 1]])
                k_ += 2
            else:
                units.append([tl])
                k_ += 1

        def issue_S(u):
            slot = rr("spair", 2)
            for q_, tl in enumerate(u):
                h_, i, t_, n0 = tl["h"], tl["i"], tl["t"], tl["n0"]
                if tl["first"]:
                    hstate[h_] = dict(bo=bank("o"))
                if i < j:
                    if t_ == 0:
                        sl = rr("kvr", NKV)
                        hstate[h_]["sl"] = sl
                        S.op("sp", lambda h, sl=sl, i=i, h_=h_: h.dma_start(out=kvr[sl][0:96, 0:TB],
                                                                           in_=k_s[l][h_][:, i * TB:(i + 1) * TB]),
                             reads=[("Ks", l, i)], writes=[("kvr", sl)], dma=True, key=("kvr", sl))
                        S.op("sp", lambda h, sl=sl, i=i, h_=h_: h.dma_start(
                            out=kvr[sl][:, TB:2 * TB].rearrange("p (t n) -> p t n", n=128),
                            in_=v_s[l][h_][:, 4 * i:4 * i + 4, :]),
                            reads=[("Vs", l, i)], writes=[("kvr", sl)], dma=True, key=("kvr", sl))
                    sl = hstate[h_]["sl"]
                    ka = kvr[sl][0:96, t_ * 128:(t_ + 1) * 128]
                    tl["va"] = kvr[sl][:, TB + t_ * 128:TB + (t_ + 1) * 128]
                    krd = [("kvr", sl)]
                    tl["vrd"] = [("kvr", sl)]
                else:
                    ka = Kt[0:96, h_, t_ * 128:(t_ + 1) * 128]
                    tl["va"] = Va[:, h_, t_, :]
                    krd = [("Kt", h_)]
                    tl["vrd"] = ["Va"]
                tl["bo"] = hstate[h_]["bo"]
                bs = 2 * slot + q_
                tl["bs"] = bs
                mm_group(ps[bs][:, n0:TB], [(ka, Qt[0:96, h_, n0:TB])], reads=krd + [("Qt", h_)], bnk=bs)

        def issue_PV(u):
            pslot = rr("ppair", 2)
            if len(u) == 2:
                b0 = u[0]["bs"]
                act(PtA[:, 2 * pslot * TB:(2 * pslot + 2) * TB], psA[:, b0 * TB:(b0 + 2) * TB], AF.Exp,
                    reads=[PS(b0), PS(b0 + 1), "negsh"], writes=[("Pt", 2 * pslot), ("Pt", 2 * pslot + 1)],
                    scale=sc, bias=negsh[:, l:l + 1])
            for q_, tl in enumerate(u):
                h_, i, n0, bs, bo = tl["h"], tl["i"], tl["n0"], tl["bs"], tl["bo"]
                pi = 2 * pslot + q_
                if len(u) == 1:
                    act(Pt[pi][:, n0:TB], ps[bs][:, n0:TB], AF.Exp, reads=[PS(bs), "negsh"], writes=[("Pt", pi)],
                        scale=sc, bias=negsh[:, l:l + 1])
                if i == j:
                    S.op("pool", lambda h, pi=pi, n0=n0: h.memset(Pt[pi][64:128, n0:n0 + 64], 0.0),
                         reads=[("Pt", pi)], writes=[("Pt", pi)])
                a_, b_ = tl["va"], Pt[pi][:, n0:TB]
                first, last = tl["first"], tl["last"]

                def pvfn(h, a_=a_, b_=b_, first=first, last=last, bo=bo, n0=n0):
                    return h.matmul(ps[bo][:, n0:TB], lhsT=a_, rhs=b_, start=first, stop=last)
                S.op("pe", pvfn, reads=tl["vrd"] + [("Pt", pi)], writes=[PS(bo)])
                if last:
                    li = rr("lns", 1)
                    ch = h_ // 2
                    if h_ % 2 == 0:
                        orow, srow = slice(0, 64), slice(64, 128)
                    else:
                        orow, srow = slice(64, 128), slice(0, 64)
                    act(lns[li][orow, :], ps[bo][srow, :], AF.Ln, reads=[PS(bo)], writes=[("lns", li)])
                    act(rinv[li][orow, :], lns[li][orow, :], AF.Exp, reads=[("lns", li)], writes=[("rinv", li)],
                        scale=-1.0)
                    tt("dve", mixT[orow, ch, :], ps[bo][orow, :], rinv[li][orow, :], ALU.mult,
                       reads=[PS(bo), ("rinv", li)], writes=[("mixT", ch)])
                    if h_ < NH - 1:
                        conv_taps(4 * (h_ + 1), min(31, 4 * (h_ + 2)))

        LAU = 1
        for idx in range(len(units) + LAU):
            if idx < len(units):
                issue_S(units[idx])
            if idx - LAU >= 0:
                issue_PV(units[idx - LAU])
        for cc in range(4):
            act(hT[:, 4 + cc, :], lat[:, cc, :], AF.Square, reads=[("lat", cc)], writes=[("hT", 4 + cc)])
            act(hT[:, cc, :], lat[:, cc, :], AF.Copy, reads=[("lat", cc)], writes=[("hT", cc)])
        b1 = bank("ss")
        mm_group(ps[b1][:], [(ones[:], hT[:, cc, :]) for cc in range(4)],
                 reads=[("hT", cc) for cc in range(4)] + ["ones"], bnk=b1)
        b2 = bank("ss")
        mm_group(ps[b2][:], [(ones[:], hT[:, 4 + cc, :]) for cc in range(4)],
                 reads=[("hT", 4 + cc) for cc in range(4)] + ["ones"], bnk=b2)
        S.op("dve", lambda h: h.tensor_scalar(out=cmean[:], in0=ps[b1][:], scalar1=1.0 / 512, scalar2=None, op0=ALU.mult),
             reads=[PS(b1)], writes=["cmean"])
        tt("dve", sil[0][:], cmean[:], cmean[:], ALU.mult, reads=["cmean"], writes=[("sil", 0)])
        stt(sil[1][:], ps[b2][:], 1.0 / 512, sil[0][:], ALU.mult, ALU.subtract, reads=[PS(b2), ("sil", 0)], writes=[("sil", 1)])
        ri = rr("rstd", 2)
        act(lnv[ri][:], sil[1][:], AF.Ln, reads=[("sil", 1), "epsT"], writes=[("lnv", ri)], bias=epsT[:])
        act(rstd[ri][:], lnv[ri][:], AF.Exp, reads=[("lnv", ri)], writes=[("rstd", ri)], scale=-0.5)
        for cc in range(4):
            ci = rr("ct", 2)
            tt("dve", ct[ci][:], lat[:, cc, :], cmean[:], ALU.subtract, reads=[("lat", cc), "cmean"], writes=[("ct", ci)])
            lg = vc["clg"] + cc
            lb = vc["clb"] + cc
            stt(ct[ci][:], ct[ci][:], vecs[:, lg:lg + 1], rstd[ri][:], ALU.mult, ALU.mult,
                reads=[("ct", ci), ("rstd", ri), "vecs"], writes=[("ct", ci)])
            act(mixT[:, 4 + cc, :], ct[ci][:], AF.Silu, reads=[("ct", ci), "vecs"], writes=[("mixT", 4 + cc)],
                bias=vecs[:, lb:lb + 1])
        st2 = ss_open(Qt, "Qt")
        for s in range(2):
            sl = load_w(wo_s[l][s].rearrange("p k n -> p (k n)"), 4096, reads=[("Wo", l, s)])
            wv = wr[sl][:].rearrange("p (k n) -> p k n", n=512)
            for dd in range(4):
                d = 4 * s + dd
                bo = bank("o")
                mm_group(ps[bo][:], [(wv[:, k, dd * 128:(dd + 1) * 128], mixT[:, k, :]) for k in range(8)],
                         reads=[("mixT", k) for k in range(8)] + [("wr", sl)], bnk=bo)
                tt("dve", xT[:, d, :], ps[bo][:], xT[:, d, :], ALU.add, reads=[PS(bo), ("xT", d)], writes=[("xT", d)])
                hook2(d, st2)
        return st2

    xin_v = xT_in.rearrange("(c p) t -> p c t", p=128)
    out_v = outT.rearrange("(c p) t -> p c t", p=128)
    rFs = [(rinv[0], ("rinv", 0)), (rF2, "rF2")]
    rP = (lns[0], ("lns", 0))
    for l in range(nlayers - 1):
        pn, fn = vec_cols(l)["postn"], vec_cols(l + 1)["f1n"]
        S.op("dve", lambda h, l=l, pn=pn, fn=fn: h.tensor_tensor(out=gpg[:, 8 * l:8 * l + 8], in0=vecs[:, pn:pn + 8],
                                                               in1=vecs[:, fn:fn + 8], op=ALU.mult),
             reads=["vecs"], writes=["gpg"])
    gflat = gT[:].rearrange("p f t -> p (f t)")
    obuf = gflat[:, 6 * TB:6 * TB + 8192].bitcast(F32).rearrange("p (c t) -> p c t", t=TB)
    qflat = Qt[:].rearrange("p h t -> p (h t)").bitcast(F32).rearrange("p (c t) -> p c t", t=TB)
    kflat = Kt[:].rearrange("p h t -> p (h t)").bitcast(F32).rearrange("p (c t) -> p c t", t=TB)

    def OB(c):
        return [("gT", 6 + 2 * c), ("gT", 7 + 2 * c)]

    def stg(c):
        return qflat[:, c, :] if c < 4 else kflat[:, c - 4, :]

    def STG(c):
        nm, cc = ("Qt", c) if c < 4 else ("Kt", c - 4)
        return [(nm, 2 * cc), (nm, 2 * cc + 1)]
    sqs = [(Pt[i][:], ("Pt", i)) for i in range(4)] + [(sql[:, i, :], ("sql", i)) for i in range(4)]
    g1c = vec_cols(0)["f1n"]

    def stn_mm(stn, c):
        b = stn["b"]
        first, last_ = (stn["n"] == 0), (stn["n"] == 7)
        ap_, reg_ = sqs[c]
        S.op("pe", lambda h: h.matmul(ps[b][:], lhsT=ones[:], rhs=ap_, start=first, stop=last_),
             reads=[reg_, "ones"], writes=[PS(b)])
        stn["n"] += 1

    def next_prep(d, stn):
        src = stg(d)
        S.op("dve", lambda h: h.tensor_scalar(out=hT[:, d, :], in0=src, scalar1=vecs[:, g1c + d:g1c + d + 1],
                                               scalar2=None, op0=ALU.mult),
             reads=STG(d) + ["vecs"], writes=[("hT", d)])
        ap_, reg_ = sqs[d]
        act(ap_, src, AF.Square, reads=STG(d), writes=[reg_])
        if stn["pend"] is not None:
            stn_mm(stn, stn["pend"])
        stn["pend"] = d

    pre = None
    for j in range(nblk):
        S.op("sp", lambda h, j=j: h.dma_start(out=ropeT[:], in_=rope_d[:, j * TB:(j + 1) * TB]), writes=["ropeT"],
             dma=True, key="rope")
        if pre is None:
            S.op("sp", lambda h, j=j: h.dma_start(out=xT[:], in_=xin_v[:, :, j * TB:(j + 1) * TB]), writes=XT,
                 dma=True, key="xin")
            rb = rFs[rr("rF", 2)]
            for c in range(8):
                hprep(c, vecs[:, g1c + c:g1c + c + 1])
            st0 = ss_open(mixT, "mixT")
            act(mixT[:], xT[:], AF.Square, reads=XT, writes=[("mixT", c) for c in range(8)])
            for c in range(7):
                ss_mm(st0, c)
            st0["pend"] = 7
            r_from(st0, rb)
        else:
            rb = pre
        pre = None
        for l in range(nlayers):
            vc = vec_cols(l)
            last = (l == nlayers - 1)
            stm = ss_open(mixT, "mixT")
            mixn = vc["mixn"]

            def hook1(d, stm=stm, mixn=mixn):
                hprep(d, vecs[:, mixn + d:mixn + d + 1])
                sq_acc(stm, d)
            ffn(0, l, vc, rb, hook1)
            f2n = vc["f2n"]

            def hook2(d, st2, f2n=f2n):
                hprep(d, vecs[:, f2n + d:f2n + d + 1])
                sq_acc(st2, d)
            st2 = mixer(l, j, vc, stm, hook2)
            rb2 = rFs[rr("rF", 2)]
            r_from(st2, rb2)
            stA = ss_open(mixT, "mixT")
            pn = vc["postn"]
            prefetch = last and (j < nblk - 1)
            stn = None
            if not last:
                stB = ss_open(Qt, "Qt")

                def hook3(d, stA=stA, stB=stB, pn=pn, l=l):
                    sq_acc(stA, d)
                    sq_acc(stB, d, scale=vecs[:, pn + d:pn + d + 1])
                    hprep(d, gpg[:, 8 * l + d:8 * l + d + 1], extra="gpg")
            elif prefetch:
                S.op("sp", lambda h, j=j: h.dma_start(out=qflat, in_=xin_v[:, 0:4, (j + 1) * TB:(j + 2) * TB]),
                     writes=[("Qt", h_) for h_ in range(NH)], dma=True, key="xin")
                S.op("sp", lambda h, j=j: h.dma_start(out=kflat, in_=xin_v[:, 4:8, (j + 1) * TB:(j + 2) * TB]),
                     writes=[("Kt", h_) for h_ in range(NH)], dma=True, key="xin")
                stn = dict(b=bank("ss"), n=0, pend=None)

                def hook3(d, stA=stA, stn=stn):
                    sq_acc(stA, d)
                    next_prep(d, stn)
            else:
                def hook3(d, stA=stA):
                    sq_acc(stA, d)
            ffn(1, l, vc, rb2, hook3)
            r_from(stA, rP)
            if not last:
                rb = rFs[rr("rF", 2)]
                ss_close(stB)
                ci = rr("ct", 2)
                tt("dve", ct[ci][:], ps[stB["b"]][:], rP[0][:], ALU.mult, reads=[PS(stB["b"]), rP[1]],
                   writes=[("ct", ci)])
                tt("dve", ct[ci][:], ct[ci][:], rP[0][:], ALU.mult, reads=[("ct", ci), rP[1]], writes=[("ct", ci)])
                act(lnv[1][:], ct[ci][:], AF.Ln, reads=[("ct", ci), "epsT"], writes=[("lnv", 1)], scale=1.0 / D,
                    bias=epsT[:])
                act(rb[0][:], lnv[1][:], AF.Exp, reads=[("lnv", 1)], writes=[rb[1]], scale=-0.5)
                tt("dve", rb[0][:], rb[0][:], rP[0][:], ALU.mult, reads=[rb[1], rP[1]], writes=[rb[1]])
                for c in range(8):
                    stt(xT[:, c, :], xT[:, c, :], vecs[:, pn + c:pn + c + 1], rP[0][:], ALU.mult, ALU.mult,
                        reads=[("xT", c), rP[1], "vecs"], writes=[("xT", c)])
            else:
                for c in range(8):
                    stt(obuf[:, c, :], xT[:, c, :], vecs[:, pn + c:pn + c + 1], rP[0][:], ALU.mult, ALU.mult,
                        reads=[("xT", c), rP[1], "vecs"], writes=OB(c))
                S.op("pool", lambda h, j=j: h.dma_start(out=out_v[:, :, j * TB:(j + 1) * TB], in_=obuf),
                     reads=[r_ for c in range(8) for r_ in OB(c)], writes=[("out", j)], dma=True, key="out")
                if prefetch:
                    stn_mm(stn, stn["pend"])
                    rbn = rFs[rr("rF", 2)]
                    act(lnv[1][:], ps[stn["b"]][:], AF.Ln, reads=[PS(stn["b"]), "epsT"], writes=[("lnv", 1)],
                        scale=1.0 / D, bias=epsT[:])
                    act(rbn[0][:], lnv[1][:], AF.Exp, reads=[("lnv", 1)], writes=[rbn[1]], scale=-0.5)
                    for c in range(8):
                        S.op("pool", lambda h, c=c: h.tensor_copy(out=xT[:, c, :], in_=stg(c)),
                             reads=STG(c), writes=[("xT", c)])
                    pre = rbn
    S.op("sp", lambda h: None, reads=[("out", j) for j in range(nblk)])
    S.emit()
    return nc


def _rope_table(T):
    pos = np.arange(T, dtype=np.float32)
    inv_freq = (1.0 / (np.float32(10000.0) ** (np.arange(0, 32, 2, dtype=np.float32) / np.float32(32)))).astype(np.float32)
    ang = (pos[:, None] * inv_freq[None, :]).astype(np.float32)
    cos = np.cos(ang).astype(np.float32).T
    sin = np.sin(ang).astype(np.float32).T
    blk = np.concatenate([cos, cos, -sin, sin], axis=0)
    return np.ascontiguousarray(np.concatenate([blk, blk], axis=0))


def _prep_shared(inp):
    f = lambda k: np.asarray(inp[k], dtype=np.float32)
    w_in = f("w_in")
    w_in_ext = np.concatenate([w_in[:, :, 0:672], w_in[:, :, 656:672], w_in[:, :, 640:656], w_in[:, :, 672:]], axis=2)
    w_uq = f("w_uq").reshape(L, 384, NH, 96)
    w_uq_ext = np.concatenate([w_uq, w_uq[..., 80:96], w_uq[..., 64:80]], axis=3).reshape(L, 384, NH * 128)
    w_ukv = f("w_ukv").reshape(L, 256, NH, 128)
    w_k = w_ukv[..., 0:64].reshape(L, 256, 512)
    w_v = w_ukv[..., 64:128].reshape(L, 256, 512)
    vecs = np.zeros((128, L * NVL), np.float32)
    qk_bc = np.zeros((128, L * 2 * 96), np.float32)
    for l in range(L):
        vc = vec_cols(l)

        def put(name, v):
            v = np.asarray(v, np.float32)
            n = v.shape[0] // 128
            vecs[:, vc[name]:vc[name] + n] = v.reshape(n, 128).T
        put("f1n", f("ffn1_norm")[l]); put("mixn", f("mix_norm")[l]); put("f2n", f("ffn2_norm")[l])
        put("postn", f("post_norm")[l]); put("qln", f("q_latent_norm")[l]); put("kvln", f("kv_latent_norm")[l])
        qn = f("q_norm")[l]
        kn = f("k_norm")[l]
        vecs[:, vc["gq"]] = np.concatenate([qn[0:96], qn[80:96], qn[64:80]])
        vecs[:, vc["gkA"]] = np.concatenate([kn[0:64], kn[0:64]])
        vecs[0:64, vc["gkpe"]] = np.concatenate([kn[64:96], kn[80:96], kn[64:80]])
        put("cb", f("conv_b")[l]); put("clg", f("conv_ln_g")[l]); put("clb", f("conv_ln_b")[l])
        cw = f("conv_w")[l]
        vecs[:, vc["cw"]:vc["cw"] + 124] = cw.reshape(31, 4, 128).transpose(2, 1, 0).reshape(128, 124)
        qk_bc[:, (2 * l) * 96:(2 * l + 1) * 96] = qn[None, :]
        qk_bc[:, (2 * l + 1) * 96:(2 * l + 2) * 96] = kn[None, :]
    c = np.ascontiguousarray

    def gu(gn, un):
        g = f(gn).reshape(L, 8, 128, NSLAB, 256)
        u = f(un).reshape(L, 8, 128, NSLAB, 256)
        a = np.stack([g, u], axis=0)
        return c(a.transpose(1, 4, 3, 0, 2, 5)).reshape(L, NSLAB, 256, 2048)

    def wd(n):
        w = f(n).reshape(L, NF, 128, 8, 128)
        return c(w.transpose(0, 3, 2, 1, 4)).reshape(L, 8, 176, 2048)

    wi = w_in_ext.reshape(L, 8, 128, 1728).transpose(0, 2, 1, 3)
    offs = [(0, 384), (384, 704), (704, 1216), (1216, 1728)]
    wins = [c(wi[:, :, :, a:b]).reshape(L, -1, 2048) for (a, b) in offs]
    wuq = c(w_uq_ext.reshape(L, 3, 128, 1024).transpose(0, 2, 1, 3)).reshape(L, 192, 2048)
    wk_ = w_k.reshape(L, 2, 128, 512)
    wv_ = w_v.reshape(L, 2, 128, 512)
    wkv = c(np.stack([wk_, wv_], axis=1).transpose(0, 3, 1, 2, 4)).reshape(L, 128, 2048)
    wo = c(f("w_out").reshape(L, 8, 128, 2, 512).transpose(0, 3, 2, 1, 4)).reshape(L, 2, 256, 2048)
    return {
        "ffn1_gu": gu("ffn1_w_gate", "ffn1_w_up"), "ffn2_gu": gu("ffn2_w_gate", "ffn2_w_up"),
        "ffn1_wd": wd("ffn1_w_down"), "ffn2_wd": wd("ffn2_w_down"),
        "win0": wins[0], "win1": wins[1], "win2": wins[2], "win3": wins[3],
        "wuq": wuq, "wkv": wkv, "wo": wo,
        "vecs": vecs, "qk_bc": qk_bc,
    }


def kernel(**inputs):
    x = np.asarray(inputs["x"], dtype=np.float32)
    B, T, _ = x.shape
    nblk = T // TB
    shared = _prep_shared(inputs)
    shared["ropeT"] = _rope_table(T)
    nc = build_nc(nblk=nblk, nlayers=L)
    in_maps = []
    for b in range(B):
        m = dict(shared)
        m["xT"] = np.ascontiguousarray(x[b].T)
        in_maps.append(m)
    res = run_bass_kernel_spmd(nc, in_maps, core_ids=list(range(B)))
    out = np.stack([np.ascontiguousarray(r["outT"].T) for r in res.results], axis=0)
    return out.astype(np.float32)
```

```python
import numpy as np
import concourse.bass as bass
import concourse.mybir as mybir
from concourse.bass_utils import run_bass_kernel_spmd

F32 = mybir.dt.float32
BF16 = mybir.dt.bfloat16
AF = mybir.ActivationFunctionType
ALU = mybir.AluOpType
AX = mybir.AxisListType

L = 2
D = 1024
FF = 2816
SEQ = 4096
TB = 512
NH = 8
EPS = 1e-6
NF = FF // 128
NSLAB = FF // 256
NVL = 176
WSLOT = 4096

ENGS = ("pe", "act", "dve", "pool", "sp")


class Op:
    __slots__ = ("eng", "fn", "reads", "writes", "dma", "key", "idx", "deps", "signal", "seq", "nwait")

    def __init__(self, eng, fn, reads, writes, dma, key):
        self.eng, self.fn, self.reads, self.writes = eng, fn, reads, writes
        self.dma, self.key = dma, key
        self.deps = []
        self.signal = False
        self.seq = 0


class Sched:
    def __init__(self, nc):
        self.nc = nc
        self.ops = []
        self.lastw = {}
        self.readers = {}
        self.dma_cnt = {}

    def op(self, eng, fn, reads=(), writes=(), dma=False, key=None):
        psr = [r for r in reads if isinstance(r, tuple) and r[0] == "ps"]
        if psr:
            reads = [r for r in reads if not (isinstance(r, tuple) and r[0] == "ps")]
            writes = list(writes) + [r for r in psr if r not in writes]
        o = Op(eng, fn, tuple(reads), tuple(writes), dma, key)
        o.idx = len(self.ops)
        deps = set()
        for r in o.reads:
            w = self.lastw.get(r)
            if w is not None:
                deps.add((w, 0))
        for r in o.writes:
            w = self.lastw.get(r)
            if w is not None:
                deps.add((w, 1))
            for rd in self.readers.get(r, ()):
                deps.add((rd, 2))
        for r in o.reads:
            self.readers.setdefault(r, []).append(o.idx)
        for r in o.writes:
            self.lastw[r] = o.idx
            self.readers[r] = []
        dd = {}
        for (d, kind) in deps:
            if d == o.idx:
                continue
            dop = self.ops[d]
            if (not dop.dma) and (not dma) and dop.eng == eng and (kind != 0 or eng == "pe"):
                continue
            dd[d] = True
        o.deps = sorted(dd.keys())
        o.nwait = {}
        for d in o.deps:
            dop = self.ops[d]
            if dop.dma:
                o.nwait[dop.key] = self.dma_cnt[dop.key]
            else:
                dop.signal = True
        if dma:
            assert key is not None
            self.dma_cnt[key] = self.dma_cnt.get(key, 0) + 1
        self.ops.append(o)
        return o

    def emit(self):
        nc = self.nc
        cnt = {e: 0 for e in ENGS}
        for o in self.ops:
            if not o.dma and o.signal:
                cnt[o.eng] += 1
                o.seq = cnt[o.eng]
        esem = {e: nc.alloc_semaphore(f"s_{e}") for e in ENGS}
        ksem = {k: nc.alloc_semaphore(f"d_{i}") for i, k in enumerate(self.dma_cnt)}
        per = {e: [o for o in self.ops if o.eng == e] for e in ENGS}
        ops = self.ops

        def run(eng_name, h):
            known = {}
            for o in per[eng_name]:
                waits = {}
                for d in o.deps:
                    dop = ops[d]
                    if dop.dma:
                        s, v = ksem[dop.key], 16 * o.nwait[dop.key]
                    else:
                        s, v = esem[dop.eng], dop.seq
                    if waits.get(id(s), (None, 0))[1] < v:
                        waits[id(s)] = (s, v)
                for s, v in waits.values():
                    if known.get(id(s), 0) < v:
                        h.wait_ge(s, v)
                        known[id(s)] = v
                ins = o.fn(h)
                if o.dma:
                    ins.then_inc(ksem[o.key], 16)
                elif o.signal:
                    ins.then_inc(esem[eng_name], 1)

        with nc.Block() as block:
            @block.tensor
            def _(h):
                run("pe", h)

            @block.scalar
            def _(h):
                run("act", h)

            @block.vector
            def _(h):
                run("dve", h)

            @block.gpsimd
            def _(h):
                run("pool", h)

            @block.sync
            def _(h):
                run("sp", h)


def vec_cols(l):
    b = l * NVL
    o = {}
    o["f1n"] = b; o["mixn"] = b + 8; o["f2n"] = b + 16; o["postn"] = b + 24
    o["qln"] = b + 32; o["kvln"] = b + 35
    o["gq"] = b + 37; o["gkA"] = b + 38; o["gkpe"] = b + 39
    o["cb"] = b + 40; o["clg"] = b + 44; o["clb"] = b + 48; o["cw"] = b + 52
    return o


def build_nc(nblk=8, nlayers=2, dbg=None):
    nc = bass.Bass("TRN2", target_bir_lowering=False)
    S = Sched(nc)
    T = nblk * TB

    def din(name, shape, dt=F32):
        return nc.dram_tensor(name, list(shape), dt, kind="ExternalInput").ap()

    xT_in = din("xT", [D, T])
    outT = nc.dram_tensor("outT", [D, T], F32, kind="ExternalOutput").ap()
    gu_in = [din("ffn1_gu", [L, NSLAB, 256, 2048]), din("ffn2_gu", [L, NSLAB, 256, 2048])]
    wd_in = [din("ffn1_wd", [L, 8, 176, 2048]), din("ffn2_wd", [L, 8, 176, 2048])]
    win_in = [din("win0", [L, 192, 2048]), din("win1", [L, 160, 2048]), din("win2", [L, 256, 2048]),
              din("win3", [L, 256, 2048])]
    wuq_in = din("wuq", [L, 192, 2048])
    wkv_in = din("wkv", [L, 128, 2048])
    wo_in = din("wo", [L, 2, 256, 2048])
    vecs_d = din("vecs", [128, L * NVL])
    qk_d = din("qk_bc", [128, L * 2 * 96])
    rope_d = din("ropeT", [128, T])

    def dscr(name, shape, dt=BF16):
        return nc.dram_tensor(name, list(shape), dt, kind="Internal").ap()

    gu_s = [[dscr(f"gu_s{t}_{l}", [NSLAB, 128, 2, 8, 256]) for l in range(L)] for t in range(2)]
    wd_s = [[dscr(f"wd_s{t}_{l}", [8, 128, NF, 128]) for l in range(L)] for t in range(2)]
    win_s = [[dscr(f"win_s{l}_0", [128, 8, 384]), dscr(f"win_s{l}_1", [128, 8, 320]),
              dscr(f"win_s{l}_2", [128, 8, 512]), dscr(f"win_s{l}_3", [128, 8, 512])] for l in range(L)]
    wuq_s = [dscr(f"wuq_s{l}", [128, 3, 1024]) for l in range(L)]
    wkv_s = [dscr(f"wkv_s{l}", [128, 2, 2, 512]) for l in range(L)]
    wo_s = [dscr(f"wo_s{l}", [2, 128, 8, 512]) for l in range(L)]
    k_s = [dscr(f"k_s{l}", [NH, 96, T]) for l in range(L)]
    v_s = [dscr(f"v_s{l}", [NH, 128, 4 * nblk, 128]) for l in range(L)]

    sb = nc.alloc_sbuf_tensor
    xT = sb("xTs", [128, 8, TB], F32)
    hT = sb("hT", [128, 8, TB], BF16)
    gT = sb("gT", [128, NF, TB], BF16)
    NW = 5
    wr = [sb(f"wr{i}", [128, WSLOT], BF16) for i in range(NW)]
    vecs = sb("vecs_sb", [128, L * NVL], F32)
    qkb = sb("qkb", [128, L * 2, 96], F32)
    qkm = sb("qkm", [128, L * 2], F32)
    negsh = sb("negsh", [128, L], F32)
    epsT = sb("epsT", [128, 1], F32)
    ones = sb("ones", [128, 128], BF16)
    bones = sb("bones", [128, 128], BF16)
    ropeT = sb("ropeTs", [128, TB], F32)
    rstd = [sb(f"rstd{i}", [128, TB], F32) for i in range(2)]
    lnv = [sb(f"lnv{i}", [128, TB], F32) for i in range(2)]
    sil = [sb(f"sil{i}", [128, TB], F32) for i in range(2)]
    lat = sb("lat", [128, 5, TB], F32)
    sql = sb("sql", [128, 5, TB], BF16)
    latn = sb("latn", [128, 5, TB], BF16)
    sqpe = sb("sqpe", [32, TB], BF16)
    kmisc = sb("kmisc", [128, 2, TB], F32)
    ktmp = kmisc[0:32]
    kr = kmisc[:, 0, :]
    rsA = kmisc[:, 1, :]
    rmix = sb("rmix", [128, TB], F32)
    rF2 = sb("rF2", [128, TB], F32)
    gpg = sb("gpg", [128, 8 * L], F32)
    sqq = [sb(f"sqq{i}", [128, TB], BF16) for i in range(2)]
    qs = [sb(f"qs{i}", [128, TB], F32) for i in range(1)]
    rt = [sb(f"rt{i}", [128, 2, TB], F32) for i in range(1)]
    Qt = sb("Qt", [128, NH, TB], BF16)
    Kt = sb("Kt", [128, NH, TB], BF16)
    Va = sb("Va", [128, NH, 4, 128], BF16)
    NKV = 6
    kvr = [sb(f"kvr{i}", [128, 1024], BF16) for i in range(NKV)]
    NPT = 4
    PtA = sb("PtA", [128, NPT * TB], BF16)
    Pt = [PtA[:, i * TB:(i + 1) * TB] for i in range(NPT)]
    lns = [sb(f"lns{i}", [128, TB], F32) for i in range(1)]
    rinv = [sb(f"rinv{i}", [128, TB], F32) for i in range(1)]
    mixT = sb("mixT", [128, 8, TB], BF16)
    ub = [sb(f"ub{l}", [128, 4, TB + 30], BF16) for l in range(L)]
    cmean = sb("cmean", [128, TB], F32)
    ct = [sb(f"ct{i}", [128, TB], F32) for i in range(2)]
    psA = nc.alloc_psum_tensor("psA", [128, 8 * TB], F32)
    ps = [psA[:, i * TB:(i + 1) * TB] for i in range(8)]
    print("sbuf bytes remaining", nc.sbuf_bytes_remaining)

    ctr = {}

    def rr(name, n):
        v = ctr.get(name, 0)
        ctr[name] = v + 1
        return v % n

    pools = {"g": [0, 1], "u": [2, 3], "o": [4, 5], "ss": [6, 7], "p4": [0, 1, 2, 3]}

    def bank(pool):
        lst = pools[pool]
        return lst[rr("ps_" + pool, len(lst))]

    def PS(b):
        return ("ps", b)

    def load_w(src_ap, n_el, reads, view=None):
        s = rr("wr", NW)
        dst = wr[s][:, 0:n_el]
        S.op("sp", lambda h: h.dma_start(out=dst, in_=src_ap), reads=reads, writes=[("wr", s)],
             dma=True, key=("wr", s))
        return s

    def mm_group(out_ap, pairs, reads, bnk):
        n = len(pairs)

        def fn(h):
            ins = None
            for i, (a, b) in enumerate(pairs):
                ins = h.matmul(out_ap, lhsT=a, rhs=b, start=(i == 0), stop=(i == n - 1))
            return ins
        S.op("pe", fn, reads=reads, writes=[PS(bnk)])

    def act(out, in_, func, reads, writes, scale=None, bias=None):
        kw = {}
        if scale is not None:
            kw["scale"] = scale
        if bias is not None:
            kw["bias"] = bias
        S.op("act", lambda h: h.activation(out=out, in_=in_, func=func, **kw), reads=reads, writes=writes)

    def stt(out, in0, scalar, in1, op0, op1, reads, writes):
        S.op("dve", lambda h: h.scalar_tensor_tensor(out=out, in0=in0, scalar=scalar, in1=in1, op0=op0, op1=op1),
             reads=reads, writes=writes)

    def tt(eng, out, in0, in1, op, reads, writes):
        S.op(eng, lambda h: h.tensor_tensor(out=out, in0=in0, in1=in1, op=op), reads=reads, writes=writes)

    XT = [("xT", c) for c in range(8)]
    HT = [("hT", c) for c in range(8)]

    def rstd_from(bnk, nparts, inv_n, reads_extra=()):
        i = rr("rstd", 2)
        act(lnv[i][0:nparts, :], ps[bnk][0:nparts, :], AF.Ln, reads=[PS(bnk), "epsT"], writes=[("lnv", i)],
            scale=inv_n, bias=epsT[0:nparts, :])
        act(rstd[i][0:nparts, :], lnv[i][0:nparts, :], AF.Exp, reads=[("lnv", i)], writes=[("rstd", i)], scale=-0.5)
        return i

    def ss_open(buf, nm):
        return dict(b=bank("ss"), n=0, pend=None, buf=buf, nm=nm)

    def ss_mm(st, c):
        b, buf, nm = st["b"], st["buf"], st["nm"]
        first, last = (st["n"] == 0), (st["n"] == 7)
        S.op("pe", lambda h: h.matmul(ps[b][:], lhsT=ones[:], rhs=buf[:, c, :], start=first, stop=last),
             reads=[(nm, c), "ones"], writes=[PS(b)])
        st["n"] += 1

    def sq_acc(st, c, scale=None):
        act(st["buf"][:, c, :], xT[:, c, :], AF.Square, reads=[("xT", c), "vecs"], writes=[(st["nm"], c)],
            scale=scale)
        if st["pend"] is not None:
            ss_mm(st, st["pend"])
        st["pend"] = c

    def ss_close(st):
        ss_mm(st, st["pend"])
        st["pend"] = None
        assert st["n"] == 8

    def norm_finish(st, gcol, inplace=False, hook=None):
        ss_close(st)
        i = rstd_from(st["b"], 128, 1.0 / D)
        for c in range(8):
            dst, nm = (xT, "xT") if inplace else (hT, "hT")
            stt(dst[:, c, :], xT[:, c, :], vecs[:, gcol + c:gcol + c + 1], rstd[i][:], ALU.mult, ALU.mult,
                reads=[("xT", c), ("rstd", i), "vecs"], writes=[(nm, c)])
            if hook is not None:
                hook(c)

    def norm_block_start(gcol):
        st = ss_open(hT, "hT")
        act(hT[:], xT[:], AF.Square, reads=XT, writes=HT)
        for c in range(7):
            ss_mm(st, c)
        st["pend"] = 7
        norm_finish(st, gcol)

    S.op("sp", lambda h: h.dma_start(out=vecs[:], in_=vecs_d), writes=["vecs"], dma=True, key="setup")
    S.op("sp", lambda h: h.dma_start(out=qkb[:], in_=qk_d.rearrange("p (a b) -> p a b", b=96)), writes=["qkb"],
         dma=True, key="setup")
    S.op("dve", lambda h: h.memset(epsT[:], EPS), writes=["epsT"])
    S.op("dve", lambda h: h.memset(ones[:], 1.0), writes=["ones"])
    S.op("dve", lambda h: h.memset(bones[:], 0.0), writes=["bones"])
    S.op("dve", lambda h: h.memset(bones[0:64, 0:64], 1.0), writes=["bones"])
    S.op("dve", lambda h: h.memset(bones[64:128, 64:128], 1.0), writes=["bones"])
    S.op("dve", lambda h: h.memset(Va[:], 1.0), writes=["Va"])
    for l in range(L):
        S.op("dve", lambda h, l=l: h.memset(ub[l][:, :, 0:30], 0.0), writes=[("ub", l)])
    S.op("dve", lambda h: h.tensor_reduce(out=qkm[:], in_=qkb[:], axis=AX.X, op=ALU.max, apply_absolute_value=True),
         reads=["qkb"], writes=["qkm"])
    qkm_v = qkm[:].rearrange("p (l t) -> p l t", t=2)
    S.op("dve", lambda h: h.tensor_tensor(out=negsh[:], in0=qkm_v[:, :, 0], in1=qkm_v[:, :, 1], op=ALU.mult),
         reads=["qkm"], writes=["negsh0"])
    S.op("dve", lambda h: h.tensor_scalar(out=negsh[:], in0=negsh[:], scalar1=-float(np.sqrt(96.0)), scalar2=None,
                                           op0=ALU.mult), reads=["negsh0"], writes=["negsh"])

    def cast(dst, src, wreg, key):
        S.op("pool", lambda h: h.dma_start(out=dst, in_=src), writes=[wreg], dma=True, key=key)

    def flat(ap_):
        names = " ".join(f"a{i}" for i in range(len(ap_.shape)))
        return ap_.rearrange(f"{names} -> ({names})").rearrange("(r n) -> r n", n=2048)

    def conv_ffn(t, l):
        for s in range(NSLAB):
            cast(flat(gu_s[t][l][s]), gu_in[t][l][s], ("Wgu", t, l, s), ("cgu", t, l, s if (t == 0 and l == 0) else 0))
        for d in range(8):
            cast(flat(wd_s[t][l][d]), wd_in[t][l][d], ("Wd", t, l, d), ("cwd", t, l, d if (t == 0 and l == 0) else 0))

    def conv_mix(l):
        for i in range(4):
            cast(flat(win_s[l][i]), win_in[i][l], ("Win", l, i), ("cmix", l))
        cast(flat(wuq_s[l]), wuq_in[l], ("Wuq", l), ("cmix", l))
        cast(flat(wkv_s[l]), wkv_in[l], ("Wkv", l), ("cmix", l))
        for s in range(2):
            cast(flat(wo_s[l][s]), wo_in[l][s], ("Wo", l, s), ("cmix", l))

    for l in range(nlayers):
        conv_ffn(0, l)
        conv_mix(l)
        conv_ffn(1, l)

    def hprep(d, gap, extra="vecs"):
        S.op("dve", lambda h: h.tensor_scalar(out=hT[:, d, :], in0=xT[:, d, :], scalar1=gap, scalar2=None,
                                               op0=ALU.mult),
             reads=[("xT", d), extra], writes=[("hT", d)])

    def r_from(st, rb):
        ss_close(st)
        act(lnv[0][:], ps[st["b"]][:], AF.Ln, reads=[PS(st["b"]), "epsT"], writes=[("lnv", 0)],
            scale=1.0 / D, bias=epsT[:])
        act(rb[0][:], lnv[0][:], AF.Exp, reads=[("lnv", 0)], writes=[rb[1]], scale=-0.5)

    def ffn(t, l, vc, rb, hook=None):
        rbuf, rname = rb
        for s in range(NSLAB):
            sl = load_w(gu_s[t][l][s].rearrange("p a c f -> p (a c f)"), 4096,
                        reads=[("Wgu", t, l, s)])
            wv = wr[sl][:].rearrange("p (a c f) -> p a c f", a=2, c=8)
            for ff in range(2):
                f = 2 * s + ff
                bg = bank("g")
                mm_group(ps[bg][:], [(wv[:, 0, c, ff * 128:(ff + 1) * 128], hT[:, c, :]) for c in range(8)],
                         reads=HT + [("wr", sl)], bnk=bg)
                bu = bank("u")
                mm_group(ps[bu][:], [(wv[:, 1, c, ff * 128:(ff + 1) * 128], hT[:, c, :]) for c in range(8)],
                         reads=HT + [("wr", sl)], bnk=bu)
                i = rr("sil", 2)
                tt("dve", sil[i][:], ps[bg][:], rbuf[:], ALU.mult, reads=[PS(bg), rname], writes=[("sil", i)])
                act(sil[i][:], sil[i][:], AF.Silu, reads=[("sil", i)], writes=[("sil", i)])
                tt("dve", gT[:, f, :], ps[bu][:], sil[i][:], ALU.mult, reads=[PS(bu), ("sil", i)], writes=[("gT", f)])
        for d in range(8):
            sl = load_w(wd_s[t][l][d].rearrange("p f n -> p (f n)"), NF * 128, reads=[("Wd", t, l, d)])
            wv = wr[sl][:, 0:NF * 128].rearrange("p (f n) -> p f n", n=128)
            bo = bank("o")
            mm_group(ps[bo][:], [(wv[:, f, :], gT[:, f, :]) for f in range(NF)],
                     reads=[("gT", f) for f in range(NF)] + [("wr", sl)], bnk=bo)
            ci = rr("ct", 2)
            tt("dve", ct[ci][:], ps[bo][:], rbuf[:], ALU.mult, reads=[PS(bo), rname], writes=[("ct", ci)])
            stt(xT[:, d, :], ct[ci][:], 0.5, xT[:, d, :], ALU.mult, ALU.add, reads=[("ct", ci), ("xT", d)],
                writes=[("xT", d)])
            if hook is not None:
                hook(d)

    def mixer(l, j, vc, stm, hook2):
        ss_close(stm)
        act(lnv[0][:], ps[stm["b"]][:], AF.Ln, reads=[PS(stm["b"]), "epsT", ("lnv", 0)], writes=[("lnv", 0)],
            scale=1.0 / D, bias=epsT[:])
        act(rmix[:], lnv[0][:], AF.Exp, reads=[("lnv", 0)], writes=["rmix"], scale=-0.5)
        if j > 0:
            S.op("pool", lambda h: h.tensor_copy(out=ub[l][:, :, 0:30], in_=ub[l][:, :, TB:TB + 30]),
                 reads=[("ub", l)], writes=[("ub", l)])
        s0 = load_w(win_s[l][0].rearrange("p c n -> p (c n)"), 8 * 384, reads=[("Win", l, 0)])
        w0 = wr[s0][:, 0:8 * 384].rearrange("p (c n) -> p c n", n=384)
        s1 = load_w(win_s[l][1].rearrange("p c n -> p (c n)"), 8 * 320, reads=[("Win", l, 1)])
        w1 = wr[s1][:, 0:8 * 320].rearrange("p (c n) -> p c n", n=320)
        for i in range(5):
            b = bank("p4")
            if i < 3:
                pairs = [(w0[:, c, i * 128:(i + 1) * 128], hT[:, c, :]) for c in range(8)]
                rd = HT + [("wr", s0)]
            else:
                pairs = [(w1[:, c, (i - 3) * 128:(i - 2) * 128], hT[:, c, :]) for c in range(8)]
                rd = HT + [("wr", s1)]
            mm_group(ps[b][:], pairs, reads=rd, bnk=b)
            tt("dve", lat[:, i, :], ps[b][:], rmix[:], ALU.mult, reads=[PS(b), "rmix"], writes=[("lat", i)])
            act(sql[:, i, :], lat[:, i, :], AF.Square, reads=[("lat", i)], writes=[("sql", i)])
        bk = bank("p4")
        mm_group(ps[bk][0:64, :], [(w1[:, c, 256:320], hT[:, c, :]) for c in range(8)], reads=HT + [("wr", s1)], bnk=bk)
        tt("dve", qs[0][0:64, :], ps[bk][0:64, :], rmix[0:64, :], ALU.mult, reads=[PS(bk), "rmix"], writes=[("qs", 0)])
        act(sqpe[:], qs[0][0:32, :], AF.Square, reads=[("qs", 0)], writes=["sqpe"])
        g0 = vc["gkpe"]
        stt(ktmp[:, 0, :], qs[0][0:32, :], vecs[0:32, g0:g0 + 1], ropeT[0:32, :], ALU.mult, ALU.mult,
            reads=[("qs", 0), "vecs", "ropeT"], writes=[("ktmp", 0)])
        stt(ktmp[:, 1, :], qs[0][32:64, :], vecs[32:64, g0:g0 + 1], ropeT[32:64, :], ALU.mult, ALU.mult,
            reads=[("qs", 0), "vecs", "ropeT"], writes=[("ktmp", 1)])
        tt("pool", kr[64:96, :], ktmp[:, 0, :], ktmp[:, 1, :], ALU.add, reads=[("ktmp", 0), ("ktmp", 1)], writes=["kr"])
        for (lo, n, gname, inv) in ((0, 3, "qln", 1.0 / 384), (3, 2, "kvln", 1.0 / 256)):
            b = bank("ss")
            mm_group(ps[b][:], [(ones[:], sql[:, lo + i, :]) for i in range(n)],
                     reads=[("sql", lo + i) for i in range(n)] + ["ones"], bnk=b)
            ri = rstd_from(b, 128, inv)
            for i in range(n):
                col = vc[gname] + i
                stt(latn[:, lo + i, :], lat[:, lo + i, :], vecs[:, col:col + 1], rstd[ri][:], ALU.mult, ALU.mult,
                    reads=[("lat", lo + i), ("rstd", ri), "vecs"], writes=[("latn", lo + i)])
        s2 = load_w(win_s[l][2].rearrange("p c n -> p (c n)"), 4096, reads=[("Win", l, 2)])
        w2 = wr[s2][:].rearrange("p (c n) -> p c n", n=512)
        s3 = load_w(win_s[l][3].rearrange("p c n -> p (c n)"), 4096, reads=[("Win", l, 3)])
        w3 = wr[s3][:].rearrange("p (c n) -> p c n", n=512)
        for cc in range(4):
            ba = bank("g")
            mm_group(ps[ba][:], [(w2[:, c, cc * 128:(cc + 1) * 128], hT[:, c, :]) for c in range(8)],
                     reads=HT + [("wr", s2)], bnk=ba)
            bg = bank("u")
            mm_group(ps[bg][:], [(w3[:, c, cc * 128:(cc + 1) * 128], hT[:, c, :]) for c in range(8)],
                     reads=HT + [("wr", s3)], bnk=bg)
            i = rr("sil", 2)
            tt("dve", sil[i][:], ps[bg][:], rmix[:], ALU.mult, reads=[PS(bg), "rmix"], writes=[("sil", i)])
            act(sil[i][:], sil[i][:], AF.Sigmoid, reads=[("sil", i)], writes=[("sil", i)])
            ci = rr("ct", 2)
            tt("dve", ct[ci][:], ps[ba][:], sil[i][:], ALU.mult, reads=[PS(ba), ("sil", i)], writes=[("ct", ci)])
            tt("pool", ub[l][:, cc, 30:30 + TB], ct[ci][:], rmix[:], ALU.mult, reads=[("ct", ci), "rmix"],
               writes=[("ub", l)])
        sq_ = load_w(wuq_s[l].rearrange("p c n -> p (c n)"), 3 * 1024, reads=[("Wuq", l)])
        wq = wr[sq_][:, 0:3 * 1024].rearrange("p (c n) -> p c n", n=1024)
        skv = load_w(wkv_s[l].rearrange("p a c n -> p (a c n)"), 2048, reads=[("Wkv", l)])
        wkv = wr[skv][:, 0:2048].rearrange("p (a c n) -> p a c n", a=2, c=2)
        gq = vc["gq"]
        gk = vc["gkA"]

        def qA(h_):
            b = bank("p4")
            mm_group(ps[b][:], [(wq[:, c, h_ * 128:(h_ + 1) * 128], latn[:, c, :]) for c in range(3)],
                     reads=[("latn", c) for c in range(3)] + [("wr", sq_)], bnk=b)
            qi = rr("sqq", 2)
            act(sqq[qi][0:96, :], ps[b][0:96, :], AF.Square, reads=[PS(b)], writes=[("sqq", qi)])
            return (b, qi)

        def qB(h_, st_):
            b, qi = st_
            bs = bank("ss")
            mm_group(ps[bs][:], [(ones[0:96, :], sqq[qi][0:96, :])], reads=[("sqq", qi), "ones"], bnk=bs)
            ri = rstd_from(bs, 128, 1.0 / 96)
            stt(Qt[0:64, h_, :], ps[b][0:64, :], vecs[0:64, gq:gq + 1], rstd[ri][0:64, :], ALU.mult, ALU.mult,
                reads=[PS(b), ("rstd", ri), "vecs"], writes=[("Qt", h_)])
            si = rr("qs", 1)
            stt(qs[si][64:128, :], ps[b][64:128, :], vecs[64:128, gq:gq + 1], rstd[ri][64:128, :], ALU.mult, ALU.mult,
                reads=[PS(b), ("rstd", ri), "vecs"], writes=[("qs", si)])
            ti = rr("rt", 1)
            tt("dve", rt[ti][64:96, 0, :], qs[si][64:96, :], ropeT[64:96, :], ALU.mult,
               reads=[("qs", si), "ropeT"], writes=[("rt", ti, 0)])
            tt("pool", rt[ti][64:96, 1, :], qs[si][96:128, :], ropeT[96:128, :], ALU.mult,
               reads=[("qs", si), "ropeT"], writes=[("rt", ti, 1)])
            tt("pool", Qt[64:96, h_, :], rt[ti][64:96, 0, :], rt[ti][64:96, 1, :], ALU.add,
               reads=[("rt", ti, 0), ("rt", ti, 1)], writes=[("Qt", h_)])

        def kA(pi):
            b = bank("p4")
            mm_group(ps[b][:], [(wkv[:, 0, c, pi * 128:(pi + 1) * 128], latn[:, 3 + c, :]) for c in range(2)],
                     reads=[("latn", 3), ("latn", 4), ("wr", skv)], bnk=b)
            qi = rr("sqq", 2)
            act(sqq[qi][:], ps[b][:], AF.Square, reads=[PS(b)], writes=[("sqq", qi)])
            return (b, qi)

        def kB(pi, st_):
            b, qi = st_
            bs = bank("ss")
            mm_group(ps[bs][:], [(bones[:], sqq[qi][:]), (ones[0:32, :], sqpe[:])],
                     reads=[("sqq", qi), "sqpe", "ones", "bones"], bnk=bs)
            ri = rstd_from(bs, 128, 1.0 / 96)
            act(rsA[64:96, :], lnv[ri][0:32, :], AF.Exp, reads=[("lnv", ri)], writes=["rsA"], scale=-0.5)
            hA, hB = 2 * pi, 2 * pi + 1
            stt(Kt[0:64, hA, :], ps[b][0:64, :], vecs[0:64, gk:gk + 1], rstd[ri][0:64, :], ALU.mult, ALU.mult,
                reads=[PS(b), ("rstd", ri), "vecs"], writes=[("Kt", hA)])
            stt(Kt[0:64, hB, :], ps[b][64:128, :], vecs[64:128, gk:gk + 1], rstd[ri][64:128, :], ALU.mult, ALU.mult,
                reads=[PS(b), ("rstd", ri), "vecs"], writes=[("Kt", hB)])
            tt("pool", Kt[64:96, hA, :], kr[64:96, :], rsA[64:96, :], ALU.mult, reads=["kr", "rsA"], writes=[("Kt", hA)])
            tt("pool", Kt[64:96, hB, :], kr[64:96, :], rstd[ri][64:96, :], ALU.mult, reads=["kr", ("rstd", ri)],
               writes=[("Kt", hB)])

        def vA(t_):
            b = bank("p4")
            mm_group(ps[b][:], [(latn[:, 3 + c, t_ * 128:(t_ + 1) * 128], wkv[:, 1, c, :]) for c in range(2)],
                     reads=[("latn", 3), ("latn", 4), ("wr", skv)], bnk=b)
            return b

        def vB(t_, b):
            pv = ps[b][:].rearrange("p (h d) -> p h d", d=64)
            act(Va[:, 0:8:2, t_, 0:64], pv[:, 0:8:2, :], AF.Copy, reads=[PS(b)], writes=["Va"])
            S.op("dve", lambda h, t_=t_, pv=pv: h.tensor_copy(out=Va[:, 1:8:2, t_, 64:128], in_=pv[:, 1:8:2, :]),
                 reads=[PS(b)], writes=["Va"])

        stages = []
        for g_ in range(4):
            stages += [(qA, qB, 2 * g_), (qA, qB, 2 * g_ + 1), (kA, kB, g_), (vA, vB, g_)]
        prev = None
        for (fa, fb, arg) in stages:
            cur = (fb, arg, fa(arg))
            if prev is not None:
                prev[0](prev[1], prev[2])
            prev = cur
        prev[0](prev[1], prev[2])
        if j < nblk - 1:
            S.op("pool", lambda h: h.dma_start(out=k_s[l][:, :, j * TB:(j + 1) * TB].rearrange("h p t -> p h t"),
                                              in_=Kt[0:96, :, :]),
                 reads=[("Kt", h_) for h_ in range(NH)], writes=[("Ks", l, j)], dma=True, key=("kvs", l, j))
            S.op("pool", lambda h: h.dma_start(out=v_s[l][:, :, 4 * j:4 * j + 4, :].rearrange("h p t n -> p h t n"),
                                              in_=Va[:]),
                 reads=["Va"], writes=[("Vs", l, j)], dma=True, key=("kvs", l, j))
        def conv_taps(k_lo, k_hi):
            for k in range(k_lo, k_hi):
                for cc in range(4):
                    wc = vc["cw"] + cc * 31 + k
                    if k == 0:
                        cbc = vc["cb"] + cc
                        S.op("dve", lambda h, cc=cc, wc=wc, cbc=cbc: h.tensor_scalar(
                            out=lat[:, cc, :], in0=ub[l][:, cc, 0:TB], scalar1=vecs[:, wc:wc + 1],
                            scalar2=vecs[:, cbc:cbc + 1], op0=ALU.mult, op1=ALU.add),
                            reads=[("ub", l), "vecs"], writes=[("lat", cc)])
                    else:
                        stt(lat[:, cc, :], ub[l][:, cc, k:k + TB], vecs[:, wc:wc + 1], lat[:, cc, :], ALU.mult, ALU.add,
                            reads=[("ub", l), "vecs", ("lat", cc)], writes=[("lat", cc)])
        conv_taps(0, 4)
        sc = float(96.0 ** -0.5)
        LA = 2
        tiles = []
        for h_ in range(NH):
            ntile = 4 * (j + 1)
            tcount = 0
            for i in range(j + 1):
                for t_ in range(4):
                    tiles.append(dict(h=h_, i=i, t=t_, n0=(t_ * 128 if i == j else 0), first=(tcount == 0),
                                      last=(tcount == ntile - 1)))
                    tcount += 1
        hstate = {}
        units = []
        k_ = 0
        while k_ < len(tiles):
            tl = tiles[k_]
            if tl["i"] < j and tl["t"] % 2 == 0:
                units.append([tiles[k_], tiles[k_ + 1]])
                k_ += 2
            else:
                units.append([tl])
                k_ += 1

        def issue_S(u):
            slot = (0, 1, 3)[rr("spair", 3)]
            for q_, tl in enumerate(u):
                h_, i, t_, n0 = tl["h"], tl["i"], tl["t"], tl["n0"]
                if tl["first"]:
                    hstate[h_] = dict(bo=bank("o"))
                if i < j:
                    if t_ == 0:
                        sl = rr("kvr", NKV)
                        hstate[h_]["sl"] = sl
                        S.op("sp", lambda h, sl=sl, i=i, h_=h_: h.dma_start(out=kvr[sl][0:96, 0:TB],
                                                                           in_=k_s[l][h_][:, i * TB:(i + 1) * TB]),
                             reads=[("Ks", l, i)], writes=[("kvr", sl)], dma=True, key=("kvr", sl))
                        S.op("sp", lambda h, sl=sl, i=i, h_=h_: h.dma_start(
                            out=kvr[sl][:, TB:2 * TB].rearrange("p (t n) -> p t n", n=128),
                            in_=v_s[l][h_][:, 4 * i:4 * i + 4, :]),
                            reads=[("Vs", l, i)], writes=[("kvr", sl)], dma=True, key=("kvr", sl))
                    sl = hstate[h_]["sl"]
                    ka = kvr[sl][0:96, t_ * 128:(t_ + 1) * 128]
                    tl["va"] = kvr[sl][:, TB + t_ * 128:TB + (t_ + 1) * 128]
                    krd = [("kvr", sl)]
                    tl["vrd"] = [("kvr", sl)]
                else:
                    ka = Kt[0:96, h_, t_ * 128:(t_ + 1) * 128]
                    tl["va"] = Va[:, h_, t_, :]
                    krd = [("Kt", h_)]
                    tl["vrd"] = ["Va"]
                tl["bo"] = hstate[h_]["bo"]
                bs = 2 * slot + q_
                tl["bs"] = bs
                mm_group(ps[bs][:, n0:TB], [(ka, Qt[0:96, h_, n0:TB])], reads=krd + [("Qt", h_)], bnk=bs)

        def issue_PV(u):
            pslot = rr("ppair", 2)
            if len(u) == 2:
                b0 = u[0]["bs"]
                act(PtA[:, 2 * pslot * TB:(2 * pslot + 2) * TB], psA[:, b0 * TB:(b0 + 2) * TB], AF.Exp,
                    reads=[PS(b0), PS(b0 + 1), "negsh"], writes=[("Pt", 2 * pslot), ("Pt", 2 * pslot + 1)],
                    scale=sc, bias=negsh[:, l:l + 1])
            for q_, tl in enumerate(u):
                h_, i, n0, bs, bo = tl["h"], tl["i"], tl["n0"], tl["bs"], tl["bo"]
                pi = 2 * pslot + q_
                if len(u) == 1:
                    act(Pt[pi][:, n0:TB], ps[bs][:, n0:TB], AF.Exp, reads=[PS(bs), "negsh"], writes=[("Pt", pi)],
                        scale=sc, bias=negsh[:, l:l + 1])
                if i == j:
                    S.op("pool", lambda h, pi=pi, n0=n0: h.memset(Pt[pi][64:128, n0:n0 + 64], 0.0),
                         reads=[("Pt", pi)], writes=[("Pt", pi)])
                a_, b_ = tl["va"], Pt[pi][:, n0:TB]
                first, last = tl["first"], tl["last"]

                def pvfn(h, a_=a_, b_=b_, first=first, last=last, bo=bo, n0=n0):
                    return h.matmul(ps[bo][:, n0:TB], lhsT=a_, rhs=b_, start=first, stop=last)
                S.op("pe", pvfn, reads=tl["vrd"] + [("Pt", pi)], writes=[PS(bo)])
                if last:
                    li = rr("lns", 1)
                    ch = h_ // 2
                    if h_ % 2 == 0:
                        orow, srow = slice(0, 64), slice(64, 128)
                    else:
                        orow, srow = slice(64, 128), slice(0, 64)
                    act(lns[li][orow, :], ps[bo][srow, :], AF.Ln, reads=[PS(bo)], writes=[("lns", li)])
                    act(rinv[li][orow, :], lns[li][orow, :], AF.Exp, reads=[("lns", li)], writes=[("rinv", li)],
                        scale=-1.0)
                    tt("dve", mixT[orow, ch, :], ps[bo][orow, :], rinv[li][orow, :], ALU.mult,
                       reads=[PS(bo), ("rinv", li)], writes=[("mixT", ch)])
                    if h_ < NH - 1:
                        conv_taps(4 * (h_ + 1), min(31, 4 * (h_ + 2)))

        LAU = 2
        for idx in range(len(units) + LAU):
            if idx < len(units):
                issue_S(units[idx])
            if idx - LAU >= 0:
                issue_PV(units[idx - LAU])
        for cc in range(4):
            act(hT[:, 4 + cc, :], lat[:, cc, :], AF.Square, reads=[("lat", cc)], writes=[("hT", 4 + cc)])
            act(hT[:, cc, :], lat[:, cc, :], AF.Copy, reads=[("lat", cc)], writes=[("hT", cc)])
        b1 = bank("ss")
        mm_group(ps[b1][:], [(ones[:], hT[:, cc, :]) for cc in range(4)],
                 reads=[("hT", cc) for cc in range(4)] + ["ones"], bnk=b1)
        b2 = bank("ss")
        mm_group(ps[b2][:], [(ones[:], hT[:, 4 + cc, :]) for cc in range(4)],
                 reads=[("hT", 4 + cc) for cc in range(4)] + ["ones"], bnk=b2)
        S.op("dve", lambda h: h.tensor_scalar(out=cmean[:], in0=ps[b1][:], scalar1=1.0 / 512, scalar2=None, op0=ALU.mult),
             reads=[PS(b1)], writes=["cmean"])
        tt("dve", sil[0][:], cmean[:], cmean[:], ALU.mult, reads=["cmean"], writes=[("sil", 0)])
        stt(sil[1][:], ps[b2][:], 1.0 / 512, sil[0][:], ALU.mult, ALU.subtract, reads=[PS(b2), ("sil", 0)], writes=[("sil", 1)])
        ri = rr("rstd", 2)
        act(lnv[ri][:], sil[1][:], AF.Ln, reads=[("sil", 1), "epsT"], writes=[("lnv", ri)], bias=epsT[:])
        act(rstd[ri][:], lnv[ri][:], AF.Exp, reads=[("lnv", ri)], writes=[("rstd", ri)], scale=-0.5)
        for cc in range(4):
            ci = rr("ct", 2)
            tt("dve", ct[ci][:], lat[:, cc, :], cmean[:], ALU.subtract, reads=[("lat", cc), "cmean"], writes=[("ct", ci)])
            lg = vc["clg"] + cc
            lb = vc["clb"] + cc
            stt(ct[ci][:], ct[ci][:], vecs[:, lg:lg + 1], rstd[ri][:], ALU.mult, ALU.mult,
                reads=[("ct", ci), ("rstd", ri), "vecs"], writes=[("ct", ci)])
            act(mixT[:, 4 + cc, :], ct[ci][:], AF.Silu, reads=[("ct", ci), "vecs"], writes=[("mixT", 4 + cc)],
                bias=vecs[:, lb:lb + 1])
        st2 = ss_open(Qt, "Qt")
        for s in range(2):
            sl = load_w(wo_s[l][s].rearrange("p k n -> p (k n)"), 4096, reads=[("Wo", l, s)])
            wv = wr[sl][:].rearrange("p (k n) -> p k n", n=512)
            for dd in range(4):
                d = 4 * s + dd
                bo = bank("o")
                mm_group(ps[bo][:], [(wv[:, k, dd * 128:(dd + 1) * 128], mixT[:, k, :]) for k in range(8)],
                         reads=[("mixT", k) for k in range(8)] + [("wr", sl)], bnk=bo)
                tt("dve", xT[:, d, :], ps[bo][:], xT[:, d, :], ALU.add, reads=[PS(bo), ("xT", d)], writes=[("xT", d)])
                hook2(d, st2)
        return st2

    xin_v = xT_in.rearrange("(c p) t -> p c t", p=128)
    out_v = outT.rearrange("(c p) t -> p c t", p=128)
    rFs = [(rinv[0], ("rinv", 0)), (rF2, "rF2")]
    rP = (lns[0], ("lns", 0))
    for l in range(nlayers - 1):
        pn, fn = vec_cols(l)["postn"], vec_cols(l + 1)["f1n"]
        S.op("dve", lambda h, l=l, pn=pn, fn=fn: h.tensor_tensor(out=gpg[:, 8 * l:8 * l + 8], in0=vecs[:, pn:pn + 8],
                                                               in1=vecs[:, fn:fn + 8], op=ALU.mult),
             reads=["vecs"], writes=["gpg"])
    gflat = gT[:].rearrange("p f t -> p (f t)")
    obuf = gflat[:, 6 * TB:6 * TB + 8192].bitcast(F32).rearrange("p (c t) -> p c t", t=TB)
    qflat = Qt[:].rearrange("p h t -> p (h t)").bitcast(F32).rearrange("p (c t) -> p c t", t=TB)
    kflat = Kt[:].rearrange("p h t -> p (h t)").bitcast(F32).rearrange("p (c t) -> p c t", t=TB)

    def OB(c):
        return [("gT", 6 + 2 * c), ("gT", 7 + 2 * c)]

    def stg(c):
        return qflat[:, c, :] if c < 4 else kflat[:, c - 4, :]

    def STG(c):
        nm, cc = ("Qt", c) if c < 4 else ("Kt", c - 4)
        return [(nm, 2 * cc), (nm, 2 * cc + 1)]
    sqs = [(Pt[i][:], ("Pt", i)) for i in range(4)] + [(sql[:, i, :], ("sql", i)) for i in range(4)]
    g1c = vec_cols(0)["f1n"]

    def stn_mm(stn, c):
        b = stn["b"]
        first, last_ = (stn["n"] == 0), (stn["n"] == 7)
        ap_, reg_ = sqs[c]
        S.op("pe", lambda h: h.matmul(ps[b][:], lhsT=ones[:], rhs=ap_, start=first, stop=last_),
             reads=[reg_, "ones"], writes=[PS(b)])
        stn["n"] += 1

    def next_prep(d, stn):
        src = stg(d)
        S.op("dve", lambda h: h.tensor_scalar(out=hT[:, d, :], in0=src, scalar1=vecs[:, g1c + d:g1c + d + 1],
                                               scalar2=None, op0=ALU.mult),
             reads=STG(d) + ["vecs"], writes=[("hT", d)])
        ap_, reg_ = sqs[d]
        act(ap_, src, AF.Square, reads=STG(d), writes=[reg_])
        if stn["pend"] is not None:
            stn_mm(stn, stn["pend"])
        stn["pend"] = d

    pre = None
    for j in range(nblk):
        S.op("sp", lambda h, j=j: h.dma_start(out=ropeT[:], in_=rope_d[:, j * TB:(j + 1) * TB]), writes=["ropeT"],
             dma=True, key="rope")
        if pre is None:
            S.op("sp", lambda h, j=j: h.dma_start(out=xT[:], in_=xin_v[:, :, j * TB:(j + 1) * TB]), writes=XT,
                 dma=True, key="xin")
            rb = rFs[rr("rF", 2)]
            for c in range(8):
                hprep(c, vecs[:, g1c + c:g1c + c + 1])
            st0 = ss_open(mixT, "mixT")
            act(mixT[:], xT[:], AF.Square, reads=XT, writes=[("mixT", c) for c in range(8)])
            for c in range(7):
                ss_mm(st0, c)
            st0["pend"] = 7
            r_from(st0, rb)
        else:
            rb = pre
        pre = None
        for l in range(nlayers):
            vc = vec_cols(l)
            last = (l == nlayers - 1)
            stm = ss_open(mixT, "mixT")
            mixn = vc["mixn"]

            def hook1(d, stm=stm, mixn=mixn):
                hprep(d, vecs[:, mixn + d:mixn + d + 1])
                sq_acc(stm, d)
            ffn(0, l, vc, rb, hook1)
            f2n = vc["f2n"]

            def hook2(d, st2, f2n=f2n):
                hprep(d, vecs[:, f2n + d:f2n + d + 1])
                sq_acc(st2, d)
            st2 = mixer(l, j, vc, stm, hook2)
            rb2 = rFs[rr("rF", 2)]
            r_from(st2, rb2)
            stA = ss_open(mixT, "mixT")
            pn = vc["postn"]
            prefetch = last and (j < nblk - 1)
            stn = None
            if not last:
                stB = ss_open(Qt, "Qt")

                def hook3(d, stA=stA, stB=stB, pn=pn, l=l):
                    sq_acc(stA, d)
                    sq_acc(stB, d, scale=vecs[:, pn + d:pn + d + 1])
                    hprep(d, gpg[:, 8 * l + d:8 * l + d + 1], extra="gpg")
            elif prefetch:
                S.op("sp", lambda h, j=j: h.dma_start(out=qflat, in_=xin_v[:, 0:4, (j + 1) * TB:(j + 2) * TB]),
                     writes=[("Qt", h_) for h_ in range(NH)], dma=True, key="xin")
                S.op("sp", lambda h, j=j: h.dma_start(out=kflat, in_=xin_v[:, 4:8, (j + 1) * TB:(j + 2) * TB]),
                     writes=[("Kt", h_) for h_ in range(NH)], dma=True, key="xin")
                stn = dict(b=bank("ss"), n=0, pend=None)

                def hook3(d, stA=stA, stn=stn):
                    sq_acc(stA, d)
                    next_prep(d, stn)
            else:
                def hook3(d, stA=stA):
                    sq_acc(stA, d)
            ffn(1, l, vc, rb2, hook3)
            r_from(stA, rP)
            if not last:
                rb = rFs[rr("rF", 2)]
                ss_close(stB)
                ci = rr("ct", 2)
                tt("dve", ct[ci][:], ps[stB["b"]][:], rP[0][:], ALU.mult, reads=[PS(stB["b"]), rP[1]],
                   writes=[("ct", ci)])
                tt("dve", ct[ci][:], ct[ci][:], rP[0][:], ALU.mult, reads=[("ct", ci), rP[1]], writes=[("ct", ci)])
                act(lnv[1][:], ct[ci][:], AF.Ln, reads=[("ct", ci), "epsT"], writes=[("lnv", 1)], scale=1.0 / D,
                    bias=epsT[:])
                act(rb[0][:], lnv[1][:], AF.Exp, reads=[("lnv", 1)], writes=[rb[1]], scale=-0.5)
                tt("dve", rb[0][:], rb[0][:], rP[0][:], ALU.mult, reads=[rb[1], rP[1]], writes=[rb[1]])
                for c in range(8):
                    stt(xT[:, c, :], xT[:, c, :], vecs[:, pn + c:pn + c + 1], rP[0][:], ALU.mult, ALU.mult,
                        reads=[("xT", c), rP[1], "vecs"], writes=[("xT", c)])
            else:
                for c in range(8):
                    stt(obuf[:, c, :], xT[:, c, :], vecs[:, pn + c:pn + c + 1], rP[0][:], ALU.mult, ALU.mult,
                        reads=[("xT", c), rP[1], "vecs"], writes=OB(c))
                S.op("pool", lambda h, j=j: h.dma_start(out=out_v[:, :, j * TB:(j + 1) * TB], in_=obuf),
                     reads=[r_ for c in range(8) for r_ in OB(c)], writes=[("out", j)], dma=True, key="out")
                if prefetch:
                    stn_mm(stn, stn["pend"])
                    rbn = rFs[rr("rF", 2)]
                    act(lnv[1][:], ps[stn["b"]][:], AF.Ln, reads=[PS(stn["b"]), "epsT"], writes=[("lnv", 1)],
                        scale=1.0 / D, bias=epsT[:])
                    act(rbn[0][:], lnv[1][:], AF.Exp, reads=[("lnv", 1)], writes=[rbn[1]], scale=-0.5)
                    for c in range(8):
                        S.op("pool", lambda h, c=c: h.tensor_copy(out=xT[:, c, :], in_=stg(c)),
                             reads=STG(c), writes=[("xT", c)])
                    pre = rbn
    S.op("sp", lambda h: None, reads=[("out", j) for j in range(nblk)])
    S.emit()
    return nc


def _rope_table(T):
    pos = np.arange(T, dtype=np.float32)
    inv_freq = (1.0 / (np.float32(10000.0) ** (np.arange(0, 32, 2, dtype=np.float32) / np.float32(32)))).astype(np.float32)
    ang = (pos[:, None] * inv_freq[None, :]).astype(np.float32)
    cos = np.cos(ang).astype(np.float32).T
    sin = np.sin(ang).astype(np.float32).T
    blk = np.concatenate([cos, cos, -sin, sin], axis=0)
    return np.ascontiguousarray(np.concatenate([blk, blk], axis=0))


def _prep_shared(inp):
    f = lambda k: np.asarray(inp[k], dtype=np.float32)
    w_in = f("w_in")
    w_in_ext = np.concatenate([w_in[:, :, 0:672], w_in[:, :, 656:672], w_in[:, :, 640:656], w_in[:, :, 672:]], axis=2)
    w_uq = f("w_uq").reshape(L, 384, NH, 96)
    w_uq_ext = np.concatenate([w_uq, w_uq[..., 80:96], w_uq[..., 64:80]], axis=3).reshape(L, 384, NH * 128)
    w_ukv = f("w_ukv").reshape(L, 256, NH, 128)
    w_k = w_ukv[..., 0:64].reshape(L, 256, 512)
    w_v = w_ukv[..., 64:128].reshape(L, 256, 512)
    vecs = np.zeros((128, L * NVL), np.float32)
    qk_bc = np.zeros((128, L * 2 * 96), np.float32)
    for l in range(L):
        vc = vec_cols(l)

        def put(name, v):
            v = np.asarray(v, np.float32)
            n = v.shape[0] // 128
            vecs[:, vc[name]:vc[name] + n] = v.reshape(n, 128).T
        put("f1n", f("ffn1_norm")[l]); put("mixn", f("mix_norm")[l]); put("f2n", f("ffn2_norm")[l])
        put("postn", f("post_norm")[l]); put("qln", f("q_latent_norm")[l]); put("kvln", f("kv_latent_norm")[l])
        qn = f("q_norm")[l]
        kn = f("k_norm")[l]
        vecs[:, vc["gq"]] = np.concatenate([qn[0:96], qn[80:96], qn[64:80]])
        vecs[:, vc["gkA"]] = np.concatenate([kn[0:64], kn[0:64]])
        vecs[0:64, vc["gkpe"]] = np.concatenate([kn[64:96], kn[80:96], kn[64:80]])
        put("cb", f("conv_b")[l]); put("clg", f("conv_ln_g")[l]); put("clb", f("conv_ln_b")[l])
        cw = f("conv_w")[l]
        vecs[:, vc["cw"]:vc["cw"] + 124] = cw.reshape(31, 4, 128).transpose(2, 1, 0).reshape(128, 124)
        qk_bc[:, (2 * l) * 96:(2 * l + 1) * 96] = qn[None, :]
        qk_bc[:, (2 * l + 1) * 96:(2 * l + 2) * 96] = kn[None, :]
    c = np.ascontiguousarray

    def gu(gn, un):
        g = f(gn).reshape(L, 8, 128, NSLAB, 256)
        u = f(un).reshape(L, 8, 128, NSLAB, 256)
        a = np.stack([g, u], axis=0)
        return c(a.transpose(1, 4, 3, 0, 2, 5)).reshape(L, NSLAB, 256, 2048)

    def wd(n):
        w = f(n).reshape(L, NF, 128, 8, 128)
        return c(w.transpose(0, 3, 2, 1, 4)).reshape(L, 8, 176, 2048)

    wi = w_in_ext.reshape(L, 8, 128, 1728).transpose(0, 2, 1, 3)
    offs = [(0, 384), (384, 704), (704, 1216), (1216, 1728)]
    wins = [c(wi[:, :, :, a:b]).reshape(L, -1, 2048) for (a, b) in offs]
    wuq = c(w_uq_ext.reshape(L, 3, 128, 1024).transpose(0, 2, 1, 3)).reshape(L, 192, 2048)
    wk_ = w_k.reshape(L, 2, 128, 512)
    wv_ = w_v.reshape(L, 2, 128, 512)
    wkv = c(np.stack([wk_, wv_], axis=1).transpose(0, 3, 1, 2, 4)).reshape(L, 128, 2048)
    wo = c(f("w_out").reshape(L, 8, 128, 2, 512).transpose(0, 3, 2, 1, 4)).reshape(L, 2, 256, 2048)
    return {
        "ffn1_gu": gu("ffn1_w_gate", "ffn1_w_up"), "ffn2_gu": gu("ffn2_w_gate", "ffn2_w_up"),
        "ffn1_wd": wd("ffn1_w_down"), "ffn2_wd": wd("ffn2_w_down"),
        "win0": wins[0], "win1": wins[1], "win2": wins[2], "win3": wins[3],
        "wuq": wuq, "wkv": wkv, "wo": wo,
        "vecs": vecs, "qk_bc": qk_bc,
    }


def kernel(**inputs):
    x = np.asarray(inputs["x"], dtype=np.float32)
    B, T, _ = x.shape
    nblk = T // TB
    shared = _prep_shared(inputs)
    shared["ropeT"] = _rope_table(T)
    nc = build_nc(nblk=nblk, nlayers=L)
    in_maps = []
    for b in range(B):
        m = dict(shared)
        m["xT"] = np.ascontiguousarray(x[b].T)
        in_maps.append(m)
    res = run_bass_kernel_spmd(nc, in_maps, core_ids=list(range(B)))
    out = np.stack([np.ascontiguousarray(r["outT"].T) for r in res.results], axis=0)
    return out.astype(np.float32)
```
